# Optimizing a Trainium2 kernel written in Bass

```python
import jax, jax.numpy as jnp
from jax import lax
import numpy as np

D_MODEL = 1024
BATCH = 8
SEQ = 8192
DEPTH = 1
DEC_BATCH = 8
DEC_SEQ = 16
PAST_LEN = 1024

CHUNK = 64
SGU_LEN = 128
SGU_GROUPS = 8
SGU_WIDTH = 512
SGU_GDIM = SGU_WIDTH // SGU_GROUPS
N_HEADS = 8
N_KV_HEADS = 4
HEAD_DIM = 64
GROUP = N_HEADS // N_KV_HEADS
ATT_WIDTH = N_HEADS * HEAD_DIM
KV_WIDTH = N_KV_HEADS * HEAD_DIM
N_IDX_HEADS = 8
IDX_DIM = 64
TOPK_MAX = 256
Q_BLOCK = 128
RMS_EPS = 1e-6
LN_EPS = 1e-5
SPLIT_SIZES = (SGU_WIDTH, SGU_WIDTH, SGU_WIDTH,
               ATT_WIDTH, KV_WIDTH, KV_WIDTH, ATT_WIDTH,
               N_IDX_HEADS * IDX_DIM, IDX_DIM, N_IDX_HEADS,
               D_MODEL, D_MODEL)
IN_WIDTH = (3 * SGU_WIDTH + 2 * ATT_WIDTH + 2 * KV_WIDTH
            + N_IDX_HEADS * IDX_DIM + IDX_DIM + N_IDX_HEADS + 2 * D_MODEL)

kernel_name = "chunk_causal_gmlp_dsa_hybrid_step"


def _rmsnorm(x, g):
    xf = x.astype(jnp.float32)
    y = xf * lax.rsqrt(jnp.mean(xf * xf, axis=-1, keepdims=True) + RMS_EPS)
    return (y * g.astype(jnp.float32)).astype(x.dtype)


def _chunk_mask(q_pos, k_pos):
    return (k_pos[None, :] // CHUNK) <= (q_pos[:, None] // CHUNK)


def _project(x, norm_g, w_in):
    h = _rmsnorm(x, norm_g)
    points = np.cumsum(SPLIT_SIZES)[:-1].tolist()
    return jnp.split(h @ w_in, points, axis=-1)


def _sgu(u_pre, v_pre, ln_g, ln_b, w_s, b_s):
    b, L, _ = u_pre.shape
    n = min(L, SGU_LEN)
    u = jax.nn.gelu(u_pre)
    v = jax.nn.gelu(v_pre)
    vf = v.astype(jnp.float32)
    mu = jnp.mean(vf, axis=-1, keepdims=True)
    var = jnp.mean(jnp.square(vf - mu), axis=-1, keepdims=True)
    vn = ((vf - mu) * lax.rsqrt(var + LN_EPS) * ln_g.astype(jnp.float32)
          + ln_b.astype(jnp.float32)).astype(v.dtype)
    pos = jnp.arange(n)
    ws = jnp.where(_chunk_mask(pos, pos)[None], w_s[:, :n, :n], 0)
    vc = vn.reshape(b, L // n, n, SGU_GROUPS, SGU_GDIM)
    mixed = jnp.einsum('gij,bcjgd->bcigd', ws, vc) + b_s[:, :n].T[None, None, :, :, None]
    return u * mixed.reshape(b, L, SGU_WIDTH), vn


def _sparse_attend(q, qi, wi, q_pos, k, v, kidx, k_pos, topk):
    b, nq = q.shape[0], q.shape[1]
    rel = jax.nn.relu(jnp.einsum('bqhd,bld->bqhl', qi, kidx) * (IDX_DIM ** -0.5))
    score = jnp.einsum('bqhl,bqh->bql', rel, wi).astype(jnp.float32) * (N_IDX_HEADS ** -0.5)
    score = jnp.where(_chunk_mask(q_pos, k_pos)[None], score, -jnp.inf)
    top_val, top_idx = lax.top_k(score, topk)
    valid = top_val > -jnp.inf
    gather = jax.vmap(lambda rows, idx: rows[idx])
    k_sel = gather(k, top_idx)
    v_sel = gather(v, top_idx)
    qg = q.reshape(b, nq, N_KV_HEADS, GROUP, HEAD_DIM)
    logits = jnp.einsum('bqhgd,bqnhd->bqhgn', qg, k_sel).astype(jnp.float32) * (HEAD_DIM ** -0.5)
    logits = jnp.where(valid[:, :, None, None, :], logits, -jnp.inf)
    p = jax.nn.softmax(logits, axis=-1).astype(v.dtype)
    out = jnp.einsum('bqhgn,bqnhd->bqhgd', p, v_sel)
    return out.reshape(b, nq, ATT_WIDTH)


def _prompt_attention(q, qi, wi, k, v, kidx):
    bsz, seq = q.shape[0], q.shape[1]
    topk = min(TOPK_MAX, seq // 4)
    k_pos = jnp.arange(seq)

    def block(i):
        start = i * Q_BLOCK
        sl = lambda a: lax.dynamic_slice_in_dim(a, start, Q_BLOCK, axis=1)
        return _sparse_attend(sl(q), sl(qi), sl(wi), start + jnp.arange(Q_BLOCK),
                              k, v, kidx, k_pos, topk)

    out = lax.map(block, jnp.arange(seq // Q_BLOCK))
    return jnp.swapaxes(out, 0, 1).reshape(bsz, seq, ATT_WIDTH)


def _merge(x, a_out, b_out, g_a, g_b, w_pa, w_pb, w_out):
    m = jax.nn.sigmoid(g_a) * (a_out @ w_pa) + jax.nn.sigmoid(g_b) * (b_out @ w_pb)
    return x + m @ w_out


def _heads(q, k, v, qi, ki, wi):
    b, L = q.shape[0], q.shape[1]
    return (q.reshape(b, L, N_HEADS, HEAD_DIM), k.reshape(b, L, N_KV_HEADS, HEAD_DIM),
            v.reshape(b, L, N_KV_HEADS, HEAD_DIM), qi.reshape(b, L, N_IDX_HEADS, IDX_DIM), ki, wi)


def setup_inputs(seed: int = 0) -> dict:
    key = jax.random.key(seed)
    ks = jax.random.split(key, 18)
    f32 = jnp.float32
    nrm = lambda k, s: jax.random.normal(k, s, f32)
    return {
        "x_prompt": nrm(ks[0], (BATCH, SEQ, D_MODEL)),
        "x_sample": nrm(ks[1], (DEC_BATCH, DEC_SEQ, D_MODEL)),
        "cache_k": nrm(ks[2], (DEPTH, DEC_BATCH, PAST_LEN, N_KV_HEADS, HEAD_DIM)),
        "cache_v": nrm(ks[3], (DEPTH, DEC_BATCH, PAST_LEN, N_KV_HEADS, HEAD_DIM)),
        "cache_kidx": nrm(ks[4], (DEPTH, DEC_BATCH, PAST_LEN, IDX_DIM)),
        "norm_g": 1.0 + 0.02 * nrm(ks[5], (DEPTH, D_MODEL)),
        "w_in": nrm(ks[6], (DEPTH, D_MODEL, IN_WIDTH)) * D_MODEL ** -0.5,
        "sgu_ln_g": 1.0 + 0.02 * nrm(ks[7], (DEPTH, SGU_WIDTH)),
        "sgu_ln_b": 0.02 * nrm(ks[8], (DEPTH, SGU_WIDTH)),
        "sgu_w": nrm(ks[9], (DEPTH, SGU_GROUPS, SGU_LEN, SGU_LEN)) * SGU_LEN ** -0.5,
        "sgu_b": 1.0 + 0.02 * nrm(ks[10], (DEPTH, SGU_GROUPS, SGU_LEN)),
        "w_pa": nrm(ks[11], (DEPTH, SGU_WIDTH, D_MODEL)) * SGU_WIDTH ** -0.5,
        "w_pb": nrm(ks[12], (DEPTH, ATT_WIDTH, D_MODEL)) * ATT_WIDTH ** -0.5,
        "w_out": nrm(ks[13], (DEPTH, D_MODEL, D_MODEL)) * D_MODEL ** -0.5,
        "final_g": 1.0 + 0.02 * nrm(ks[14], (D_MODEL,)),
    }


def reference(x_prompt, x_sample, cache_k, cache_v, cache_kidx, norm_g, w_in, sgu_ln_g, sgu_ln_b,
              sgu_w, sgu_b, w_pa, w_pb, w_out, final_g):
    xp, xs = x_prompt, x_sample
    pk, pv, pki, sk, sv, ski, svn = [], [], [], [], [], [], []
    topk_s = min(TOPK_MAX, (PAST_LEN + DEC_SEQ) // 4)
    k_pos_s = jnp.arange(PAST_LEN + DEC_SEQ)
    q_pos_s = PAST_LEN + jnp.arange(DEC_SEQ)
    for l in range(DEPTH):
        u, v, za, q, k, vv, zb, qi, ki, wi, ga, gb = _project(xp, norm_g[l], w_in[l])
        a_out, _ = _sgu(u, v, sgu_ln_g[l], sgu_ln_b[l], sgu_w[l], sgu_b[l])
        a_out = a_out * jax.nn.silu(za)
        q4, k4, v4, qi4, ki4, wi4 = _heads(q, k, vv, qi, ki, wi)
        b_out = _prompt_attention(q4, qi4, wi4, k4, v4, ki4) * jax.nn.silu(zb)
        xp = _merge(xp, a_out, b_out, ga, gb, w_pa[l], w_pb[l], w_out[l])
        pk.append(k4); pv.append(v4); pki.append(ki4)
        u, v, za, q, k, vv, zb, qi, ki, wi, ga, gb = _project(xs, norm_g[l], w_in[l])
        a_out, vn_s = _sgu(u, v, sgu_ln_g[l], sgu_ln_b[l], sgu_w[l], sgu_b[l])
        a_out = a_out * jax.nn.silu(za)
        q4, k4, v4, qi4, ki4, wi4 = _heads(q, k, vv, qi, ki, wi)
        k_all = jnp.concatenate([cache_k[l], k4], axis=1)
        v_all = jnp.concatenate([cache_v[l], v4], axis=1)
        ki_all = jnp.concatenate([cache_kidx[l], ki4], axis=1)
        b_out = _sparse_attend(q4, qi4, wi4, q_pos_s, k_all, v_all, ki_all, k_pos_s, topk_s)
        b_out = b_out * jax.nn.silu(zb)
        xs = _merge(xs, a_out, b_out, ga, gb, w_pa[l], w_pb[l], w_out[l])
        sk.append(k4); sv.append(v4); ski.append(ki4); svn.append(vn_s)
    y_prompt = _rmsnorm(xp, final_g)
    y_sample = _rmsnorm(xs, final_g)
    prompt_k = jnp.stack(pk)
    prompt_v = jnp.stack(pv)
    prompt_kidx = jnp.stack(pki)
    sample_k = jnp.stack(sk)
    sample_v = jnp.stack(sv)
    sample_kidx = jnp.stack(ski)
    sample_sgu_v = jnp.stack(svn)
    return (y_prompt, y_sample, prompt_k, prompt_v, prompt_kidx, sample_k, sample_v, sample_kidx, sample_sgu_v)
```

```python
import numpy as np
from contextlib import ExitStack
import concourse.bass as bass
import concourse.mybir as mybir
from concourse.bass_utils import run_bass_kernel_spmd

F32 = mybir.dt.float32
BF16 = mybir.dt.bfloat16
AF = mybir.ActivationFunctionType
ALU = mybir.AluOpType

D = 1024
RMS_EPS = 1e-6
LN_EPS = 1e-5
NEG = -1.0e30
TOPK = 256
C_UVZ, C_KV, C_KIWI, C_Q, C_QI, C_KK, C_ZB, C_GA, C_GB, C_END = (
    0, 1536, 2048, 2120, 2632, 3144, 3272, 3784, 4808, 5832)
Q_HEAD_ORDER = [0, 2, 1, 3, 4, 6, 5, 7]
BIS_W = 16.0
BIS_ITERS = 21


class _Op:
    __slots__ = ("eng", "emit", "deps", "sig", "val", "slot", "phase")


class Prog:
    ENG = ("pe", "act", "dve", "pool", "sp")

    def __init__(self):
        self.ops = {e: [] for e in self.ENG}
        self.lastw = {}
        self.readers = {}
        self.slots = {}
        self.slot_last = {}
        self.phase = 0
        self.waited = {e: {} for e in self.ENG}
        self.barrier = None

    def add(self, eng, emit, reads=(), writes=(), slot=None):
        op = _Op()
        op.eng, op.emit, op.sig, op.val, op.slot, op.phase = eng, emit, False, None, slot, self.phase
        is_dma = slot is not None
        deps = {}
        for k in reads:
            w = self.lastw.get(k)
            if w is not None:
                deps[id(w)] = (w, "raw")
        for k in writes:
            w = self.lastw.get(k)
            if w is not None:
                deps[id(w)] = (w, "waw")
            for r in self.readers.get(k, {}).values():
                if id(r) not in deps:
                    deps[id(r)] = (r, "war")
        final = []
        for d, kind in deps.values():
            d_dma = d.slot is not None
            if (not is_dma) and (not d_dma) and d.eng == eng:
                if eng == "pe":
                    continue
            final.append(d)
        if is_dma and slot in self.slot_last:
            pl = self.slot_last[slot]
            if all(pl is not d for d in final):
                final.append(pl)
        if self.barrier is not None and eng in self.barrier:
            final.extend(self.barrier.pop(eng))
        for d in final:
            d.sig = True
        op.deps = final
        for k in reads:
            self.readers.setdefault(k, {})[("dma", id(op)) if is_dma else eng] = op
        for k in writes:
            self.lastw[k] = op
            self.readers[k] = {}
        if is_dma:
            c = self.slots.get(slot, 0) + 16
            self.slots[slot] = c
            op.val = c
            op.sig = True
            self.slot_last[slot] = op
        self.ops[eng].append(op)
        return op

    def set_barrier(self):
        lasts = []
        for e in ("pe", "act", "dve", "pool"):
            if self.ops[e]:
                lasts.append(self.ops[e][-1])
        lasts.extend(self.slot_last.values())
        self.barrier = {e: list(lasts) for e in self.ENG}
        for e in self.ENG:
            self.barrier[e] = [d for d in lasts if not (d.slot is None and d.eng == e)]

    def finalize(self):
        for e in ("pe", "act", "dve", "pool"):
            c = 0
            for op in self.ops[e]:
                if op.sig:
                    c += 1
                    op.val = c

    def emit_engine(self, e, h, sems, phase, final_wait=False):
        waited = self.waited[e]
        for op in self.ops[e]:
            if op.phase != phase:
                continue
            for d in op.deps:
                key = d.slot if d.slot is not None else d.eng
                if waited.get(key, 0) >= d.val:
                    continue
                h.wait_ge(sems[key], d.val)
                waited[key] = d.val
            ins = op.emit(h)
            if op.slot is not None:
                ins.then_inc(sems[op.slot], 16)
            elif op.sig:
                ins.then_inc(sems[e], 1)
        if final_wait and e == "sp":
            for s, v in self.slots.items():
                if waited.get(s, 0) < v:
                    h.wait_ge(sems[s], v)
                    waited[s] = v


def build(NB=64):
    NBT = NB + 1
    SEQ = NB * 128
    LK = max(SEQ, 1152)
    nc = bass.Bass("TRN2", target_bir_lowering=False)

    def din(name, shape, dt=F32):
        return nc.dram_tensor(name, list(shape), dt, kind="ExternalInput").ap()

    def dout(name, shape):
        return nc.dram_tensor(name, list(shape), F32, kind="ExternalOutput").ap()

    def dscr(name, shape, dt):
        return nc.dram_tensor(name, list(shape), dt).ap()

    xp = din("xp", [SEQ, D]); xs = din("xs", [128, D])
    ck = din("ck", [1024, 256]); cv = din("cv", [1024, 256]); cki = din("cki", [1024, 64])
    w_in_l = din("w_in_l", [128, 8, C_END]); w_pa_l = din("w_pa_l", [128, 4, D])
    w_pb_l = din("w_pb_l", [128, 4, D]); w_out_l = din("w_out_l", [128, 8, D])
    gcol_d = din("gcol", [128, 8]); fg_d = din("fg_bc", [128, D])
    lng_d = din("lng_bc", [128, 512]); lnb_d = din("lnb_bc", [128, 512]); bexp_d = din("bexp", [128, 512])
    wsT_d = din("wsT", [128, 8, 128]); mkp_d = din("mask_p", [128, 128]); mks_d = din("mask_s", [128, 128])
    ident_d = din("ident", [128, 128])

    y_p = dout("y_p", [SEQ, D]); y_s = dout("y_s", [16, D])
    o_pk = dout("o_pk", [SEQ, 256]); o_pv = dout("o_pv", [SEQ, 256]); o_pki = dout("o_pki", [SEQ, 64])
    o_sk = dout("o_sk", [16, 256]); o_sv = dout("o_sv", [16, 256]); o_ski = dout("o_ski", [16, 64])
    o_svn = dout("o_svn", [16, 512])

    qT_d = dscr("qT_d", [NBT, 128, 512], BF16); qiT_d = dscr("qiT_d", [NBT, 128, 512], BF16)
    wi_d = dscr("wi_d", [NBT, 128, 8], F32); szb_d = dscr("szb_d", [NBT, 128, 512], BF16)
    sgb_d = dscr("sgb_d", [NBT, 128, 1024], BF16); mat_d = dscr("mat_d", [NBT, 128, 1024], BF16)
    KT_d = dscr("KT_d", [128, 2, SEQ], BF16); VA_d = dscr("VA_d", [NB, 128, 260], BF16)
    KIT_d = dscr("KIT_d", [128, SEQ], BF16)
    KTs_d = dscr("KTs_d", [128, 2, 1152], BF16); VAs_d = dscr("VAs_d", [9, 128, 260], BF16)
    KITs_d = dscr("KITs_d", [128, 1152], BF16)

    P = Prog()
    A = P.add

    with ExitStack() as es0:
        class _Sems(dict):
            def __missing__(self, k):
                v = es0.enter_context(nc.semaphore("s_" + k))
                self[k] = v
                return v
        sems = _Sems()
        ps = es0.enter_context(nc.psum_tensor("ps", [128, 4096], F32))
        psb = ps.bitcast(BF16)

        def bank(k, n=512, p0=0, p1=128):
            return ps[p0:p1, 512 * k:512 * k + n]

        def bankb(k, n=1024):
            return psb[:, 1024 * k:1024 * k + n]

        def sb(es, name, shape, dt):
            return es.enter_context(nc.sbuf_tensor("t_" + name, list(shape), dt))

        ident_f = sb(es0, "ident_f", [128, 128], F32)
        ident_b = sb(es0, "ident_b", [128, 128], BF16)
        mhalf = sb(es0, "mhalf", [128, 1], F32)

        with ExitStack() as es1:
            W = sb(es1, "W", [128, 8, C_END], BF16)
            wpa = sb(es1, "wpa", [128, 4, D], BF16)
            stg = sb(es1, "stg", [128, 8, 512], F32)
            gcol = sb(es1, "gcol", [128, 8], F32)
            lng = sb(es1, "lng", [128, 512], F32); lnb = sb(es1, "lnb", [128, 512], F32)
            bexp = sb(es1, "bexp", [128, 512], F32)
            wsT_f = sb(es1, "wsT_f", [128, 8, 128], F32)
            mk_f = sb(es1, "mk_f", [128, 2, 128], F32)
            wsT_p = sb(es1, "wsT_p", [128, 8, 128], BF16); wsT_s = sb(es1, "wsT_s", [128, 8, 128], BF16)
            xt = [sb(es1, "xt%d" % i, [128, D], F32) for i in range(2)]
            junk = sb(es1, "junk", [128, D], BF16)
            xnb = [sb(es1, "xn%d" % i, [128, D], BF16) for i in range(2)]; hTb = [sb(es1, "hT%d" % i, [128, 8, 128], BF16) for i in range(2)]
            ssq = sb(es1, "ssq", [128, 1], F32); ms = sb(es1, "ms", [128, 1], F32); rstd = sb(es1, "rstd", [128, 1], F32)
            uvb = [sb(es1, "uv%d" % i, [128, 1024], F32) for i in range(2)]; tmpA = sb(es1, "tmpA", [128, 1024], F32)
            zab = [sb(es1, "za%d" % i, [128, 512], F32) for i in range(2)]; kvt = sb(es1, "kvt", [128, 512], F32)
            kiwi = sb(es1, "kiwi", [128, 72], F32); wi_st = sb(es1, "wi_st", [128, 8], F32)
            VAst = sb(es1, "VAst", [128, 4, 65], BF16)
            qT_st = sb(es1, "qT_st", [128, 512], BF16); qiT_st = sb(es1, "qiT_st", [128, 512], BF16)
            KTb = sb(es1, "KTb", [128, 2, 128], BF16); KITb = sb(es1, "KITb", [128, 128], BF16)
            tz = sb(es1, "tz", [128, 512], F32); szb_st = sb(es1, "szb_st", [128, 512], BF16)
            sgab = [sb(es1, "sga%d" % i, [128, 1024], F32) for i in range(2)]; sgbf = sb(es1, "sgbf", [128, 1024], F32)
            sgb_st = sb(es1, "sgb_st", [128, 1024], BF16); mat_st = sb(es1, "mat_st", [128, 1024], BF16)
            stats = sb(es1, "stats", [128, 6], F32); mv = sb(es1, "mv", [128, 2], F32)
            ms2 = sb(es1, "ms2", [128, 1], F32); rstd2 = sb(es1, "rstd2", [128, 1], F32)
            vn = sb(es1, "vn", [128, 512], F32); vnb = sb(es1, "vnb", [128, 512], BF16)
            sz = sb(es1, "sz", [128, 512], F32); a1 = sb(es1, "a1", [128, 512], F32)
            abf = sb(es1, "abf", [128, 512], BF16); aT = sb(es1, "aT", [128, 4, 128], BF16)
            ckt = sb(es1, "ckt", [128, 256], F32); cvt = sb(es1, "cvt", [128, 256], F32)
            ckit = sb(es1, "ckit", [128, 64], F32)
            k16 = sb(es1, "k16", [128, 256], BF16); kk16 = sb(es1, "kk16", [128, 128], BF16)

            _bk = [0]

            def nbank():
                _bk[0] = (_bk[0] + 1) % 8
                return _bk[0]

            def nbank2():
                _bk[0] = ((_bk[0] // 2 + 1) % 4) * 2 + 1
                return _bk[0] - 1

            A("sp", lambda e: e.dma_start(out=ident_f[:], in_=ident_d), [], ["ident_f"], "ldc_id")
            A("dve", lambda e: e.tensor_copy(ident_b[:], ident_f[:]), ["ident_f"], ["ident_b"])
            A("pool", lambda e: e.memset(mhalf[:], -0.5), [], ["mhalf"])
            A("pool", lambda e: e.memset(VAst[:, :, 64:65], 1.0), [], ["VAst1"])
            for nm, t, d_ in (("gcol", gcol, gcol_d), ("lng", lng, lng_d), ("lnb", lnb, lnb_d),
                              ("bexp", bexp, bexp_d), ("wsT_f", wsT_f, wsT_d)):
                A("sp", lambda e, t=t, d_=d_: e.dma_start(out=t[:], in_=d_), [], [nm], "ldc_" + nm)
            A("sp", lambda e: e.dma_start(out=mk_f[:, 0, :], in_=mkp_d), [], ["mk_f0"], "ldc_m0")
            A("sp", lambda e: e.dma_start(out=mk_f[:, 1, :], in_=mks_d), [], ["mk_f1"], "ldc_m1")
            A("dve", lambda e: e.tensor_tensor(out=wsT_p[:], in0=wsT_f[:],
                                               in1=mk_f[:, 0:1, :].to_broadcast([128, 8, 128]), op=ALU.mult),
              ["wsT_f", "mk_f0"], ["wsT_p"])
            A("dve", lambda e: e.tensor_tensor(out=wsT_s[:], in0=wsT_f[:],
                                               in1=mk_f[:, 1:2, :].to_broadcast([128, 8, 128]), op=ALU.mult),
              ["wsT_f", "mk_f1"], ["wsT_s"])
            c0 = 0
            ci = 0
            while c0 < C_END:
                n = min(512, C_END - c0)
                A("sp", lambda e, c0=c0, n=n: e.dma_start(out=stg[:, :, 0:n], in_=w_in_l[:, :, c0:c0 + n]),
                  [], ["stg"], "ldw")
                for c in range(8):
                    if c % 2 == 0:
                        A("dve", lambda e, c=c, c0=c0, n=n: e.tensor_scalar(
                            out=W[:, c, c0:c0 + n], in0=stg[:, c, 0:n], scalar1=gcol[:, c:c + 1], scalar2=None,
                            op0=ALU.mult), ["stg", "gcol"], [("W", ci)])
                    else:
                        A("act", lambda e, c=c, c0=c0, n=n: e.activation(
                            out=W[:, c, c0:c0 + n], in_=stg[:, c, 0:n], func=AF.Copy, scale=gcol[:, c:c + 1]),
                          ["stg", "gcol"], [("W", ci)])
                c0 += n
                ci += 1
            NWC = ci
            Wkeys = [("W", i) for i in range(NWC)]
            A("sp", lambda e: e.dma_start(out=stg[:, 0:4, :], in_=w_pa_l[:, :, 0:512]),
              [], ["stg"], "ldw")
            A("dve", lambda e: e.tensor_copy(wpa[:, :, 0:512], stg[:, 0:4, :]), ["stg"], ["wpa0"])
            A("sp", lambda e: e.dma_start(out=stg[:, 0:4, :], in_=w_pa_l[:, :, 512:1024]), [], ["stg"], "ldw")
            A("dve", lambda e: e.tensor_copy(wpa[:, :, 512:1024], stg[:, 0:4, :]), ["stg"], ["wpa1"])

            cur = {"s": 0}

            def mm_tok(bk, n, c0):
                hT = hTb[cur["s"]]
                for c in range(8):
                    A("pe", lambda e, c=c: e.matmul(bank(bk, n), hT[:, c, :], W[:, c, c0:c0 + n],
                                                    start=(c == 0), stop=(c == 7)),
                      [("hT", cur["s"])] + Wkeys, [("ps", bk)])

            def mm_feat(bk0, nchunks, c0, m=128):
                hT = hTb[cur["s"]]
                for j in range(nchunks):
                    bk = bk0 + (j * 128) // 512
                    off = (j * 128) % 512
                    for c in range(8):
                        A("pe", lambda e, c=c, j=j, bk=bk, off=off: e.matmul(
                            ps[0:m, 512 * bk + off:512 * bk + off + 128], W[:, c, c0 + j * m:c0 + (j + 1) * m],
                            hT[:, c, :], start=(c == 0), stop=(c == 7)),
                          [("hT", cur["s"])] + Wkeys, [("ps", bk)])

            deferred1 = []
            dstage = [0]

            def Ad(*a, **k):
                deferred1.append((dstage[0], a, k))

            def flush1(stage):
                while deferred1 and (stage is None or deferred1[0][0] <= stage):
                    _, a, k = deferred1.pop(0)
                    P.add(*a, **k)

            def p1_front_a(s, x_t):
                xk = ("x", s)
                xn = xnb[s]
                A("act", lambda e: e.activation(out=junk[:], in_=x_t[:], func=AF.Square, accum_out=ssq[:, 0:1]),
                  [xk], ["junk", "ssq"])
                A("dve", lambda e: e.tensor_scalar(out=ms[:], in0=ssq[:], scalar1=1.0 / D, scalar2=RMS_EPS,
                                                   op0=ALU.mult, op1=ALU.add), ["ssq"], ["ms"])
                A("pool", lambda e: e.tensor_tensor(out=rstd[:], in0=ms[:], in1=mhalf[:], op=ALU.pow),
                  ["ms", "mhalf"], ["rstd"])
                A("act", lambda e: e.activation(out=xn[:], in_=x_t[:], func=AF.Copy, scale=rstd[:, 0:1]),
                  [xk, "rstd"], [("xn", s)])

            def p1_front_b(s):
                xn = xnb[s]
                hT_ = hTb[s]
                bk = nbank()
                for c in range(8):
                    A("pe", lambda e, c=c, bk=bk: e.transpose(bankb(bk)[:, c * 128:(c + 1) * 128],
                                                              xn[:, c * 128:(c + 1) * 128], ident_b[:]),
                      [("xn", s), "ident_b"], [("ps", bk)])
                A("dve", lambda e, bk=bk: e.tensor_copy(hT_[:].rearrange("p a b -> p (a b)"), bankb(bk)),
                  [("ps", bk)], [("hT", s)])

            def p1_block(bi, x_ap, s, sample, nxt):
                xk = ("x", s)
                x_t = xt[s]
                uv = uvb[s]; za = zab[s]; sga = sgab[s]
                K_uv0, K_uv1, K_za, K_sga = ("uv0", s), ("uv1", s), ("za", s), ("sga", s)
                cur["s"] = s
                if nxt is not None:
                    p1_front_a(1 - s, xt[1 - s])
                b0 = nbank(); mm_tok(b0, 512, C_UVZ)
                A("act", lambda e, b0=b0: e.copy(uv[:, 0:512], bank(b0)), [("ps", b0)], [K_uv0])
                flush1(0)
                b1 = nbank(); mm_tok(b1, 512, C_UVZ + 512)
                A("dve", lambda e, b1=b1: e.tensor_copy(uv[:, 512:1024], bank(b1)), [("ps", b1)], [K_uv1])
                b2 = nbank(); mm_tok(b2, 512, C_UVZ + 1024)
                A("act", lambda e, b2=b2: e.copy(za[:], bank(b2)), [("ps", b2)], [K_za])
                flush1(0)
                b3 = nbank(); mm_tok(b3, 512, C_KV)
                A("dve", lambda e, b3=b3: e.tensor_copy(kvt[:], bank(b3)), [("ps", b3)], ["kvt"])
                b4 = nbank(); mm_tok(b4, 72, C_KIWI)
                A("act", lambda e, b4=b4: e.copy(kiwi[:], bank(b4, 72)), [("ps", b4)], ["kiwi"])
                if not sample:
                    r0 = bi * 128
                    A("sp", lambda e: e.dma_start(out=o_pk[r0:r0 + 128, :], in_=kvt[:, 0:256]), ["kvt"], [], "st_ok")
                    A("sp", lambda e: e.dma_start(out=o_pv[r0:r0 + 128, :], in_=kvt[:, 256:512]), ["kvt"], [], "st_ov")
                    A("sp", lambda e: e.dma_start(out=o_pki[r0:r0 + 128, :], in_=kiwi[:, 0:64]), ["kiwi"], [], "st_oki")
                else:
                    A("sp", lambda e: e.dma_start(out=o_sk, in_=kvt[0:16, 0:256]), ["kvt"], [], "st_ok")
                    A("sp", lambda e: e.dma_start(out=o_sv, in_=kvt[0:16, 256:512]), ["kvt"], [], "st_ov")
                    A("sp", lambda e: e.dma_start(out=o_ski, in_=kiwi[0:16, 0:64]), ["kiwi"], [], "st_oki")
                A("dve", lambda e: e.tensor_scalar(out=wi_st[:], in0=kiwi[:, 64:72], scalar1=8.0 ** -0.5, scalar2=None,
                                                   op0=ALU.mult), ["kiwi"], ["wi_st"])
                A("sp", lambda e: e.dma_start(out=wi_d[bi], in_=wi_st[:]), ["wi_st"], [("wi_d", bi)], "st_wi")
                A("pool", lambda e: e.tensor_copy(VAst[:, :, 0:64], kvt[:, 256:512].rearrange("p (g d) -> p g d", g=4)),
                  ["kvt"], ["VAst"])
                if not sample:
                    A("sp", lambda e: e.dma_start(out=VA_d[bi], in_=VAst[:].rearrange("p g d -> p (g d)")),
                      ["VAst", "VAst1"], [("VA_d", bi)], "st_va")
                else:
                    A("sp", lambda e: e.dma_start(out=VAs_d[8], in_=VAst[:].rearrange("p g d -> p (g d)")),
                      ["VAst", "VAst1"], [("VAs_d", 8)], "st_va")
                flush1(0)
                bq = nbank(); mm_feat(bq, 4, C_Q)
                A("act", lambda e, bq=bq: e.mul(qT_st[:], bank(bq), 0.125), [("ps", bq)], ["qT_st"])
                A("sp", lambda e: e.dma_start(out=qT_d[bi], in_=qT_st[:]), ["qT_st"], [("qT_d", bi)], "st_q")
                flush1(0)
                bkk = nbank(); mm_feat(bkk, 2, C_KV)
                A("dve", lambda e, bkk=bkk: e.tensor_copy(KTb[:].rearrange("p a b -> p (a b)"), bank(bkk, 256)),
                  [("ps", bkk)], ["KTb"])
                if not sample:
                    A("sp", lambda e: e.dma_start(out=KT_d[:, :, bi * 128:(bi + 1) * 128], in_=KTb[:]),
                      ["KTb"], [("KT_d", bi)], "st_kt")
                else:
                    A("sp", lambda e: e.dma_start(out=KTs_d[:, :, 1024:1152], in_=KTb[:]), ["KTb"], [("KTs_d", 8)], "st_kt")
                flush1(1)
                bqi = nbank(); mm_feat(bqi, 4, C_QI)
                A("act", lambda e, bqi=bqi: e.mul(qiT_st[:], bank(bqi), 0.125), [("ps", bqi)], ["qiT_st"])
                A("sp", lambda e: e.dma_start(out=qiT_d[bi], in_=qiT_st[:]), ["qiT_st"], [("qiT_d", bi)], "st_qi")
                flush1(1)
                bki = nbank(); mm_feat(bki, 1, C_KK)
                A("dve", lambda e, bki=bki: e.tensor_copy(KITb[:], bank(bki, 128)), [("ps", bki)], ["KITb"])
                if not sample:
                    A("sp", lambda e: e.dma_start(out=KIT_d[:, bi * 128:(bi + 1) * 128], in_=KITb[:]),
                      ["KITb"], [("KIT_d", bi)], "st_kit")
                else:
                    A("sp", lambda e: e.dma_start(out=KITs_d[:, 1024:1152], in_=KITb[:]), ["KITb"], [("KITs_d", 8)], "st_kit")
                flush1(2)
                bz = nbank(); mm_tok(bz, 512, C_ZB)
                A("act", lambda e, bz=bz: e.activation(out=tz[:], in_=bank(bz), func=AF.Tanh, scale=0.5),
                  [("ps", bz)], ["tz"])
                A("dve", lambda e: e.tensor_scalar(out=tz[:], in0=tz[:], scalar1=0.5, scalar2=0.5, op0=ALU.mult, op1=ALU.add),
                  ["tz"], ["tz"])
                A("dve", lambda e, bz=bz: e.tensor_tensor(out=szb_st[:], in0=bank(bz), in1=tz[:], op=ALU.mult),
                  [("ps", bz), "tz"], ["szb_st"])
                A("sp", lambda e: e.dma_start(out=szb_d[bi], in_=szb_st[:]), ["szb_st"], [("szb_d", bi)], "st_szb")
                flush1(2)
                bga = nbank2(); mm_tok(bga, 512, C_GA); mm_tok(bga + 1, 512, C_GA + 512)
                gps = ps[:, 512 * bga:512 * bga + 1024]
                A("act", lambda e: e.activation(out=sga[:], in_=gps, func=AF.Tanh, scale=0.5),
                  [("ps", bga), ("ps", bga + 1)], [K_sga])
                A("pool", lambda e: e.tensor_scalar(out=sga[:], in0=sga[:], scalar1=0.5, scalar2=0.5, op0=ALU.mult, op1=ALU.add),
                  [K_sga], [K_sga])
                flush1(3)
                bgb = nbank2(); mm_tok(bgb, 512, C_GB); mm_tok(bgb + 1, 512, C_GB + 512)
                gps2 = ps[:, 512 * bgb:512 * bgb + 1024]
                A("act", lambda e: e.activation(out=sgbf[:], in_=gps2, func=AF.Tanh, scale=0.5),
                  [("ps", bgb), ("ps", bgb + 1)], ["sgbf"])
                A("pool", lambda e: e.tensor_scalar(out=sgb_st[:], in0=sgbf[:], scalar1=0.5, scalar2=0.5, op0=ALU.mult, op1=ALU.add),
                  ["sgbf"], ["sgb_st"])
                A("sp", lambda e: e.dma_start(out=sgb_d[bi], in_=sgb_st[:]), ["sgb_st"], [("sgb_d", bi)], "st_sgb")
                flush1(None)
                dstage[0] = 0
                Ad("act", lambda e: e.activation(out=tmpA[:], in_=uv[:], func=AF.Square), [K_uv0, K_uv1], ["tmpA"])
                Ad("dve", lambda e: e.tensor_scalar(out=tmpA[:], in0=tmpA[:], scalar1=0.044715, scalar2=1.0,
                                                   op0=ALU.mult, op1=ALU.add), ["tmpA"], ["tmpA"])
                Ad("dve", lambda e: e.tensor_tensor(out=tmpA[:], in0=tmpA[:], in1=uv[:], op=ALU.mult),
                  ["tmpA", K_uv0, K_uv1], ["tmpA"])
                Ad("act", lambda e: e.activation(out=tmpA[:], in_=tmpA[:], func=AF.Tanh, scale=0.7978845608028654),
                  ["tmpA"], ["tmpA"])
                Ad("dve", lambda e: e.scalar_tensor_tensor(out=uv[:], in0=tmpA[:], scalar=1.0, in1=uv[:],
                                                          op0=ALU.add, op1=ALU.mult), ["tmpA", K_uv0, K_uv1], [K_uv0, K_uv1])
                Ad("dve", lambda e: e.bn_stats(out=stats[:], in_=uv[:, 512:1024]), [K_uv1], ["stats"])
                Ad("dve", lambda e: e.bn_aggr(out=mv[:], in_=stats[:]), ["stats"], ["mv"])
                Ad("dve", lambda e: e.tensor_scalar(out=ms2[:], in0=mv[:, 1:2], scalar1=4.0 * LN_EPS, scalar2=None, op0=ALU.add),
                  ["mv"], ["ms2"])
                Ad("pool", lambda e: e.tensor_tensor(out=rstd2[:], in0=ms2[:], in1=mhalf[:], op=ALU.pow),
                  ["ms2", "mhalf"], ["rstd2"])
                Ad("dve", lambda e: e.tensor_scalar(out=vn[:], in0=uv[:, 512:1024], scalar1=mv[:, 0:1], scalar2=rstd2[:, 0:1],
                                                   op0=ALU.subtract, op1=ALU.mult), [K_uv1, "mv", "rstd2"], ["vn"])
                Ad("pool", lambda e: e.tensor_tensor(out=vn[:], in0=vn[:], in1=lng[:], op=ALU.mult), ["vn", "lng"], ["vn"])
                Ad("dve", lambda e: e.tensor_tensor(out=vn[:], in0=vn[:], in1=lnb[:], op=ALU.add), ["vn", "lnb"], ["vn"])
                Ad("act", lambda e: e.copy(vnb[:], vn[:]), ["vn"], ["vnb"])
                if sample:
                    Ad("sp", lambda e: e.dma_start(out=o_svn, in_=vn[0:16, :]), ["vn"], [], "st_svn")
                dstage[0] = 1
                wsm = wsT_s if sample else wsT_p
                wk = "wsT_s" if sample else "wsT_p"
                bs = nbank()
                for g in range(8):
                    Ad("pe", lambda e, g=g, bs=bs: e.matmul(bank(bs)[:, g * 64:(g + 1) * 64], wsm[:, g, :],
                                                           vnb[:, g * 64:(g + 1) * 64], start=True, stop=True),
                      [wk, "vnb"], [("ps", bs)])
                Ad("act", lambda e: e.activation(out=sz[:], in_=za[:], func=AF.Tanh, scale=0.5), [K_za], ["sz"])
                Ad("pool", lambda e: e.tensor_scalar(out=sz[:], in0=sz[:], scalar1=0.25, scalar2=0.25, op0=ALU.mult, op1=ALU.add),
                  ["sz"], ["sz"])
                Ad("pool", lambda e: e.tensor_tensor(out=sz[:], in0=sz[:], in1=za[:], op=ALU.mult), ["sz", K_za], ["sz"])
                Ad("dve", lambda e, bs=bs: e.tensor_tensor(out=a1[:], in0=bank(bs), in1=bexp[:], op=ALU.add),
                  [("ps", bs), "bexp"], ["a1"])
                Ad("dve", lambda e: e.tensor_tensor(out=a1[:], in0=a1[:], in1=uv[:, 0:512], op=ALU.mult), ["a1", K_uv0], ["a1"])
                Ad("dve", lambda e: e.tensor_tensor(out=abf[:], in0=a1[:], in1=sz[:], op=ALU.mult), ["a1", "sz"], ["abf"])
                dstage[0] = 2
                bt = nbank()
                for c in range(4):
                    Ad("pe", lambda e, c=c, bt=bt: e.transpose(bankb(bt)[:, c * 128:(c + 1) * 128],
                                                              abf[:, c * 128:(c + 1) * 128], ident_b[:]),
                      ["abf", "ident_b"], [("ps", bt)])
                Ad("act", lambda e, bt=bt: e.copy(aT[:].rearrange("p a b -> p (a b)"), bankb(bt, 512)), [("ps", bt)], ["aT"])
                dstage[0] = 3
                bpa = nbank2()
                for nh in range(2):
                    for c in range(4):
                        Ad("pe", lambda e, c=c, nh=nh: e.matmul(
                            bank(bpa + nh), aT[:, c, :], wpa[:, c, nh * 512:(nh + 1) * 512],
                            start=(c == 0), stop=(c == 3)), ["wpa0", "wpa1", "aT"], [("ps", bpa + nh)])
                Ad("dve", lambda e: e.tensor_tensor(out=mat_st[:], in0=ps[:, 512 * bpa:512 * bpa + 1024], in1=sga[:], op=ALU.mult),
                  [("ps", bpa), ("ps", bpa + 1), K_sga], ["mat_st"])
                Ad("sp", lambda e: e.dma_start(out=mat_d[bi], in_=mat_st[:]), ["mat_st"], [("mat_d", bi)], "st_mat")

            for cb in range(8):
                r0 = cb * 128
                A("sp", lambda e, r0=r0: e.dma_start(out=ckt[:], in_=ck[r0:r0 + 128, :]), [], ["ckt"], "ldck")
                A("sp", lambda e, r0=r0: e.dma_start(out=cvt[:], in_=cv[r0:r0 + 128, :]), [], ["cvt"], "ldcv")
                A("sp", lambda e, r0=r0: e.dma_start(out=ckit[:], in_=cki[r0:r0 + 128, :]), [], ["ckit"], "ldcki")
                A("act", lambda e: e.copy(k16[:], ckt[:]), ["ckt"], ["k16"])
                bk = nbank()
                for c in range(2):
                    A("pe", lambda e, c=c, bk=bk: e.transpose(bankb(bk)[:, c * 128:(c + 1) * 128],
                                                              k16[:, c * 128:(c + 1) * 128], ident_b[:]),
                      ["k16", "ident_b"], [("ps", bk)])
                A("dve", lambda e, bk=bk: e.tensor_copy(KTb[:].rearrange("p a b -> p (a b)"), bankb(bk, 256)),
                  [("ps", bk)], ["KTb"])
                A("sp", lambda e, r0=r0: e.dma_start(out=KTs_d[:, :, r0:r0 + 128], in_=KTb[:]), ["KTb"], [("KTs_d", cb)], "st_kt")
                A("pool", lambda e: e.tensor_copy(VAst[:, :, 0:64], cvt[:].rearrange("p (g d) -> p g d", g=4)), ["cvt"], ["VAst"])
                A("sp", lambda e, cb=cb: e.dma_start(out=VAs_d[cb], in_=VAst[:].rearrange("p g d -> p (g d)")),
                  ["VAst", "VAst1"], [("VAs_d", cb)], "st_va")
                A("act", lambda e: e.copy(kk16[:, 0:64], ckit[:]), ["ckit"], ["kk16a"])
                A("act", lambda e: e.copy(kk16[:, 64:128], ckit[:]), ["ckit"], ["kk16b"])
                bk2 = nbank()
                A("pe", lambda e, bk2=bk2: e.transpose(bankb(bk2)[:, 0:128], kk16[:], ident_b[:]),
                  ["kk16a", "kk16b", "ident_b"], [("ps", bk2)])
                A("dve", lambda e, bk2=bk2: e.tensor_copy(KITb[:], bankb(bk2, 128)), [("ps", bk2)], ["KITb"])
                A("sp", lambda e, r0=r0: e.dma_start(out=KITs_d[:, r0:r0 + 128], in_=KITb[:]), ["KITb"], [("KITs_d", cb)], "st_kit")

            blocks = [(NB, xs, True)] + [(i, xp[i * 128:(i + 1) * 128, :], False) for i in range(NB)]
            A("sp", lambda e: e.dma_start(out=xt[0][:], in_=blocks[0][1]), [], [("x", 0)], "ldx0")
            PIPE_FRONT = True
            if PIPE_FRONT:
                p1_front_a(0, xt[0])
            for n_, (bi, x_ap, sample) in enumerate(blocks):
                s = n_ % 2
                nxt = None
                if n_ + 1 < len(blocks):
                    nx = blocks[n_ + 1][1]
                    nxt = nx if PIPE_FRONT else None
                    A("sp", lambda e, nx=nx, s=s: e.dma_start(out=xt[1 - s][:], in_=nx), [], [("x", 1 - s)], "ldx%d" % (1 - s))
                if not PIPE_FRONT:
                    p1_front_a(s, xt[s])
                p1_front_b(s)
                p1_block(bi, x_ap, s, sample, nxt)
            flush1(None)

            for e_ in ("pe", "act", "dve", "pool"):
                if P.ops[e_]:
                    P.ops[e_][-1].sig = True
            P.finalize()
            _p1_vals = {e: (max([op.val for op in P.ops[e] if op.val is not None] or [0])) for e in ("pe", "act", "dve", "pool")}
            with nc.Block() as blk:
                @blk.tensor
                def _(h):
                    P.emit_engine("pe", h, sems, 0)

                @blk.scalar
                def _(h):
                    P.emit_engine("act", h, sems, 0)

                @blk.vector
                def _(h):
                    P.emit_engine("dve", h, sems, 0)

                @blk.gpsimd
                def _(h):
                    P.emit_engine("pool", h, sems, 0)

                @blk.sync
                def _(h):
                    P.emit_engine("sp", h, sems, 0, final_wait=True)

        P2 = Prog()
        P2.slots = dict(P.slots)
        P2.waited = P.waited
        P2.lastw = {k: v for k, v in P.lastw.items() if isinstance(k, tuple) and isinstance(k[0], str) and k[0].endswith("_d")}
        base_vals = _p1_vals
        lasts = []
        for e in ("pe", "act", "dve", "pool"):
            if P.ops[e]:
                o = _Op(); o.eng, o.slot, o.val, o.sig = e, None, base_vals[e], True
                lasts.append(o)
        for s_, v_ in P.slots.items():
            o = _Op(); o.eng, o.slot, o.val, o.sig = "sp", s_, v_, True
            lasts.append(o)
        P2.barrier = {e: [d for d in lasts if not (d.slot is None and d.eng == e)] for e in Prog.ENG}

        sink = [None]
        pend_mrg = []
        pend_I = []

        mstage = [0]

        def A2(*a, **k):
            if sink[0] is None:
                return P2.add(*a, **k)
            if sink[0] is pend_I:
                sink[0].append((a, k))
            else:
                sink[0].append((mstage[0], a, k))

        def flushl(lst, nmax):
            n_ = len(lst) if nmax is None else min(nmax, len(lst))
            for _ in range(n_):
                a, k = lst.pop(0)
                P2.add(*a, **k)

        def flushm(stage):
            while pend_mrg and (stage is None or pend_mrg[0][0] <= stage):
                _, a, k = pend_mrg.pop(0)
                P2.add(*a, **k)

        def hook(it):
            flushm(it // 3)
            flushl(pend_I, NI_PER_IT)

        with ExitStack() as es2:
            KT = sb(es2, "KT", [128, 2, LK], BF16)
            VA = sb(es2, "VA", [128, LK // 128, 260], BF16)
            KIT = sb(es2, "KIT", [128, LK], BF16)
            SW = max(128 * (NB + 2), 1152)
            S = sb(es2, "S", [128, SW], F32)
            M = sb(es2, "M", [128, SW], BF16)
            wpb = sb(es2, "wpb", [128, 4, D], BF16); wout = sb(es2, "wout", [128, 8, D], BF16)
            fg = sb(es2, "fg", [128, D], F32)
            qT = sb(es2, "qT", [128, 512], BF16); qiT = sb(es2, "qiT", [128, 4, 128], BF16)
            wi = sb(es2, "wi", [128, 8], F32)
            szb = sb(es2, "szb", [128, 512], BF16); sgb = sb(es2, "sgb", [128, 1024], BF16)
            matt = sb(es2, "matt", [128, 1024], BF16); x2 = sb(es2, "x2", [128, D], F32)
            diag = sb(es2, "diag", [128, 8, 128], BF16)
            R = [sb(es2, "R%d" % i, [128, 512], BF16) for i in range(6)]
            Pt = [sb(es2, "Pt%d" % i, [128, 1024], BF16) for i in range(3)]
            Pm = [sb(es2, "Pm%d" % i, [128, 1024], BF16) for i in range(3)]
            MTp = [sb(es2, "MTp%d" % i, [128, 8, 128], BF16) for i in range(2)]
            sm = sb(es2, "sm", [128, 32], F32)
            mx8 = sm[:, 0:8]; tt = sm[:, 8:9]; cnt = sm[:, 9:10]; dd = sm[:, 10:11]
            sgn = sm[:, 11:12]; uu = sm[:, 12:13]; thr = sm[:, 13:14]; thr_c = sm[:, 14:15]
            ssq2 = sm[:, 15:16]; msq = sm[:, 16:17]; rs2 = sm[:, 17:18]; mone = sm[:, 18:19]
            ones_r = sb(es2, "ones_r", [128, 64], F32)
            boT = sb(es2, "boT", [128, 4, 128], BF16); btok = sb(es2, "btok", [128, 512], BF16)
            rdn = sb(es2, "rdn", [128, 8], F32)
            t1 = sb(es2, "t1", [128, 1024], F32)
            T1K = ["t1", "t1a", "t1b"]

            A2("pool", lambda e: e.memset(thr_c[:], -1.0e29), [], ["thr_c"])
            A2("pool", lambda e: e.memset(ones_r[:], 1.0), [], ["ones_r"])
            A2("pool", lambda e: e.memset(mone[:], -1.0), [], ["mone"])
            A2("sp", lambda e: e.dma_start(out=fg[:], in_=fg_d), [], ["fg"], "ldc_fg")
            for hh in range(4):
                A2("sp", lambda e, hh=hh: e.dma_start(out=t1[:], in_=w_pb_l[:, hh, :]), [], T1K, "ldw")
                A2("dve", lambda e, hh=hh: e.tensor_copy(wpb[:, hh, :], t1[:]), T1K, ["wpb"])
            for c in range(8):
                A2("sp", lambda e, c=c: e.dma_start(out=t1[:], in_=w_out_l[:, c, :]), [], T1K, "ldw")
                A2("dve", lambda e, c=c: e.tensor_copy(wout[:, c, :], t1[:]), T1K, ["wout"])

            B_S = (0, 1); B_ACC = 2; B_QK = (3, 4); B_PV = (6, 7); B_X = 4
            QK_PAIRS = ((3, 4), (0, 1))
            B_ACC2 = (2, 5)
            B_S4 = (0, 1, 3, 4)
            QK3 = ((0, 1), (2, 3), (4, 5))
            cnts = {"s": 0, "r": 0, "qk": 0, "p": 0, "mt": 0}

            STG = 9

            def p2_block(bi, sample, par):
                nkb = 9 if sample else bi + 1
                L = nkb * 128
                so = 0 if par == 0 else SW - L
                if sample:
                    A2("sp", lambda e: e.dma_start(out=KT[:, :, 0:1152], in_=KTs_d), [("KTs_d", i) for i in range(9)],
                       [("KT", i) for i in range(9)], "ld_kt")
                    A2("sp", lambda e: e.dma_start(out=VA[:, 0:9, :], in_=VAs_d.rearrange("i p d -> p i d")),
                       [("VAs_d", i) for i in range(9)], [("VA", i) for i in range(9)], "ld_va")
                    A2("sp", lambda e: e.dma_start(out=KIT[:, 0:1152], in_=KITs_d), [("KITs_d", i) for i in range(9)],
                       [("KIT", i) for i in range(9)], "ld_kit")
                A2("sp", lambda e: e.dma_start(out=qiT[:].rearrange("p a b -> p (a b)"), in_=qiT_d[bi]), [("qiT_d", bi)], ["qiT"], "ld_qi")
                A2("sp", lambda e: e.dma_start(out=wi[:], in_=wi_d[bi]), [("wi_d", bi)], ["wi"], "ld_wi")
                xsrc = xs if sample else xp[bi * 128:(bi + 1) * 128, :]

                def loads_b():
                    A2("sp", lambda e: e.dma_start(out=qT[:], in_=qT_d[bi]), [("qT_d", bi)], ["qT"], "ld_q")
                    A2("sp", lambda e: e.dma_start(out=szb[:], in_=szb_d[bi]), [("szb_d", bi)], ["szb"], "ld_szb")
                    A2("sp", lambda e: e.dma_start(out=sgb[:], in_=sgb_d[bi]), [("sgb_d", bi)], ["sgb"], "ld_sgb")
                    A2("sp", lambda e: e.dma_start(out=matt[:], in_=mat_d[bi]), [("mat_d", bi)], ["matt"], "ld_mat")
                    A2("sp", lambda e: e.dma_start(out=x2[:], in_=xsrc), [], ["x2"], "ld_x2")
                for h in range(8):
                    eng = "pool" if h % 2 else "dve"
                    A2(eng, lambda e, h=h: e.tensor_scalar(out=diag[:, h, :], in0=ident_b[:], scalar1=wi[:, h:h + 1], scalar2=0.0,
                                                          op0=ALU.mult, op1=ALU.add), ["wi", "ident_b"], [("diag", h)])
                items = []
                for kt in range((L + 511) // 512):
                    k0 = kt * 512
                    n = min(512, L - k0)
                    for h in range(8):
                        items.append((kt, k0, n, h))

                def idx_mm1(ii):
                    kt, k0, n, h = items[ii]
                    bs_ = B_S4[(h % 2) + 2 * ((ii // 2) % 2)]
                    p0 = 64 * (h % 2)
                    kbs = range(k0 // 128, (k0 + n) // 128)
                    A2("pe", lambda e: e.matmul(bank(bs_, n), qiT[p0:p0 + 64, h // 2, :], KIT[p0:p0 + 64, k0:k0 + n],
                                                start=True, stop=True),
                       ["qiT"] + [("KIT", kb) for kb in kbs], [("ps", bs_)])
                    r_ = cnts["r"] % 6; cnts["r"] += 1
                    if h % 4 != 1:
                        A2("act", lambda e: e.activation(out=R[r_][:, 0:n], in_=bank(bs_, n), func=AF.Relu),
                           [("ps", bs_)], [("R", r_)])
                    else:
                        A2("dve", lambda e: e.tensor_scalar(out=R[r_][:, 0:n], in0=bank(bs_, n), scalar1=0.0, scalar2=None,
                                                            op0=ALU.max), [("ps", bs_)], [("R", r_)])
                    return r_

                def idx_mm2(ii, r_):
                    kt, k0, n, h = items[ii]
                    ba = B_ACC2[kt % 2]
                    A2("pe", lambda e: e.matmul(bank(ba, n), diag[:, h, :], R[r_][:, 0:n], start=(h == 0), stop=(h == 7)),
                       [("diag", h), ("R", r_)], [("ps", ba)])
                    if h == 7:
                        if kt % 2 == 0:
                            A2("act", lambda e: e.copy(S[:, so + k0:so + k0 + n], bank(ba, n)), [("ps", ba)], [("S", par, kt)])
                        else:
                            A2("dve", lambda e: e.tensor_copy(S[:, so + k0:so + k0 + n], bank(ba, n)), [("ps", ba)], [("S", par, kt)])

                rr = {}
                npair = len(items) // 2
                AHP = 2
                for pj in range(npair + AHP):
                    if pj < npair:
                        rr[2 * pj] = idx_mm1(2 * pj)
                        rr[2 * pj + 1] = idx_mm1(2 * pj + 1)
                    if pj >= AHP:
                        q_ = pj - AHP
                        idx_mm2(2 * q_, rr[2 * q_])
                        idx_mm2(2 * q_ + 1, rr[2 * q_ + 1])
                SK = [("S", par, kt) for kt in range((L + 511) // 512)]
                KMd, KMa, KM = ("Md", par), ("Ma", par), ("M", par)
                if STG < 2:
                    return
                if sample:
                    A2("pool", lambda e: e.memset(S[:, so + 1040:so + 1152], NEG), SK, SK)
                else:
                    A2("pool", lambda e: e.memset(S[0:64, so + L - 64:so + L], NEG), SK, SK)
                yield "I"
                if L <= TOPK:
                    thr_t, thr_k = thr_c, "thr_c"
                else:
                    A2("dve", lambda e: e.max(out=mx8[:], in_=S[:, so:so + L]), SK, ["mx8"])
                    w = BIS_W / 2.0
                    A2("dve", lambda e, w=w: e.tensor_scalar(out=tt[:], in0=mx8[:, 0:1], scalar1=-BIS_W + w, scalar2=None, op0=ALU.add),
                       ["mx8"], ["tt"])
                    if L >= 768:
                        Ld = int(round(L * 0.47 / 128.0)) * 128
                    else:
                        Ld = L
                    nA = L - Ld
                    for it in range(BIS_ITERS):
                        wn = w / 2.0
                        A2("dve", lambda e: e.tensor_scalar(out=M[:, so:so + Ld], in0=S[:, so:so + Ld], scalar1=tt[:, 0:1], scalar2=None,
                                                            op0=ALU.is_ge, op1=ALU.add, accum_out=cnt[:, 0:1]),
                           SK + ["tt"], [KMd, "cnt"])
                        if nA > 0:
                            A2("act", lambda e: e.activation(out=M[:, so + Ld:so + L], in_=S[:, so + Ld:so + L], func=AF.Sign, bias=tt[:, 0:1], scale=-1.0,
                                                             accum_out=sgn[:, 0:1]), SK + ["tt"], [KMa, "sgn"])
                            A2("dve", lambda e: e.scalar_tensor_tensor(out=uu[:], in0=cnt[:], scalar=2.0, in1=sgn[:],
                                                                       op0=ALU.mult, op1=ALU.subtract), ["cnt", "sgn"], ["uu"])
                            A2("dve", lambda e, wn=wn: e.tensor_scalar(out=dd[:], in0=uu[:], scalar1=510.5 - nA, scalar2=2.0 * wn,
                                                                      op0=ALU.is_ge, op1=ALU.mult), ["uu"], ["dd"])
                        else:
                            A2("dve", lambda e, wn=wn: e.tensor_scalar(out=dd[:], in0=cnt[:], scalar1=float(TOPK), scalar2=2.0 * wn,
                                                                      op0=ALU.is_ge, op1=ALU.mult), ["cnt"], ["dd"])
                        A2("dve", lambda e, wn=wn: e.scalar_tensor_tensor(out=tt[:], in0=dd[:], scalar=-wn, in1=tt[:],
                                                                         op0=ALU.add, op1=ALU.add), ["dd", "tt"], ["tt"])
                        w = wn
                        hook(it)
                    A2("dve", lambda e, w=w: e.tensor_scalar(out=thr[:], in0=tt[:], scalar1=-w, scalar2=None, op0=ALU.add),
                       ["tt"], ["thr"])
                    thr_t, thr_k = thr, "thr"
                A2("dve", lambda e: e.tensor_scalar(out=M[:, so:so + L], in0=S[:, so:so + L], scalar1=thr_t[:, 0:1], scalar2=None, op0=ALU.is_ge),
                   SK + [thr_k], [KMd, KMa, KM])
                flushm(None)
                flushl(pend_I, None)
                if STG < 3:
                    return
                loads_b()
                qT4 = qT[:].rearrange("p (a j q) -> p a j q", a=2, j=2)
                first = [True, True]
                npc = (nkb + 7) // 8
                mt_slot = {}

                def att_mt(pc):
                    kb0 = pc * 8
                    nj = min(8, nkb - kb0)
                    ms_ = cnts["mt"] % 2; cnts["mt"] += 1
                    mt_slot[pc] = ms_
                    for j in range(nj):
                        A2("pe", lambda e, j=j: e.transpose(bankb(B_X)[:, j * 128:(j + 1) * 128],
                                                            M[:, so + (kb0 + j) * 128:so + (kb0 + j + 1) * 128], ident_b[:]),
                           [KM, "ident_b"], [("ps", B_X)])
                    A2("act", lambda e: e.copy(MTp[ms_][:].rearrange("p a b -> p (a b)")[:, 0:nj * 128],
                                               bankb(B_X, nj * 128)), [("ps", B_X)], [("MT", ms_)])

                def att_qk(kb):
                    pair = QK3[kb % 3]
                    r_ = kb % 3
                    for p in range(2):
                        for gi in range(2):
                            p0 = 64 * gi
                            A2("pe", lambda e, p=p, gi=gi, p0=p0: e.matmul(
                                bank(pair[gi])[:, p * 256:(p + 1) * 256], KT[p0:p0 + 64, p, kb * 128:(kb + 1) * 128],
                                qT4[p0:p0 + 64, p, :, :], start=True, stop=True),
                               [("KT", kb), "qT"], [("ps", pair[gi])])
                    A2("act", lambda e: e.activation(out=Pt[r_][:], in_=ps[:, 512 * pair[0]:512 * pair[0] + 1024], func=AF.Exp),
                       [("ps", pair[0]), ("ps", pair[1])], [("Pt", r_)])
                    ms_ = mt_slot[kb // 8]
                    j = kb % 8
                    A2("dve", lambda e: e.tensor_tensor(
                        out=Pm[r_][:].rearrange("p (h q) -> p h q", h=8), in0=Pt[r_][:].rearrange("p (h q) -> p h q", h=8),
                        in1=MTp[ms_][:, j:j + 1, :].to_broadcast([128, 8, 128]), op=ALU.mult),
                       [("Pt", r_), ("MT", ms_)], [("Pm", r_)])

                def att_pv(kb):
                    r_ = kb % 3
                    for h in range(8):
                        p, gi, j = h // 4, (h % 4) // 2, h % 2
                        g = h // 2
                        bkv = B_PV[h // 4]
                        st = first[h // 4]
                        first[h // 4] = False
                        col = gi * 512 + p * 256 + j * 128
                        A2("pe", lambda e, h=h, g=g, bkv=bkv, st=st, col=col: e.matmul(
                            ps[:, 512 * bkv + (h % 4) * 65:512 * bkv + (h % 4) * 65 + 65],
                            Pm[r_][:, col:col + 128], VA[:, kb, g * 65:(g + 1) * 65],
                            start=st, stop=(kb == nkb - 1 and h % 4 == 3), skip_group_check=True),
                           [("VA", kb), ("Pm", r_)], [("ps", bkv)])

                att_mt(0)
                att_qk(0)
                if nkb > 1:
                    att_qk(1)
                for kb in range(nkb):
                    if kb % 8 == 0 and kb // 8 + 1 < npc:
                        att_mt(kb // 8 + 1)
                    if kb + 2 < nkb:
                        att_qk(kb + 2)
                    att_pv(kb)
                if STG < 4:
                    return
                mrg_new = []
                sink[0] = mrg_new
                mstage[0] = 0
                for p in range(2):
                    pv3 = ps[:, 512 * B_PV[p]:512 * B_PV[p] + 260].rearrange("q (h d) -> q h d", h=4)
                    A2("dve", lambda e, p=p, pv3=pv3: e.reciprocal(out=rdn[:, 4 * p:4 * p + 4].rearrange("q (h o) -> q h o", o=1),
                                                                    in_=pv3[:, :, 64:65]),
                        [("ps", B_PV[p])], [("rdn", p)])
                    A2("dve", lambda e, p=p, pv3=pv3: e.tensor_tensor(
                        out=t1[:, p * 256:(p + 1) * 256].rearrange("q (h d) -> q h d", h=4), in0=pv3[:, :, 0:64],
                        in1=rdn[:, 4 * p:4 * p + 4].rearrange("q (h o) -> q h o", o=1).to_broadcast([128, 4, 64]), op=ALU.mult),
                        [("ps", B_PV[p]), ("rdn", p)], ["t1a"])
                A2("pool", lambda e: e.tensor_tensor(out=btok[:], in0=t1[:, 0:512], in1=szb[:], op=ALU.mult),
                    ["t1a", "szb"], ["btok"])
                mstage[0] = 1
                for c in range(4):
                    A2("pe", lambda e, c=c: e.transpose(bankb(7)[:, c * 128:(c + 1) * 128], btok[:, c * 128:(c + 1) * 128], ident_b[:]),
                        ["btok", "ident_b"], [("ps", 7)])
                A2("act", lambda e: e.copy(boT[:].rearrange("p a b -> p (a b)"), bankb(7, 512)), [("ps", 7)], ["boT"])
                if STG < 5:
                    return
                mstage[0] = 2
                for nh in range(2):
                    bk = 6 + nh
                    for c in range(4):
                        A2("pe", lambda e, nh=nh, c=c, bk=bk: e.matmul(
                            bank(bk), boT[:, c, :], wpb[:, c, nh * 512:(nh + 1) * 512],
                            start=(c == 0), stop=(c == 3)), ["wpb", "boT"], [("ps", bk)])
                A2("dve", lambda e: e.tensor_tensor(out=t1[:], in0=ps[:, 512 * 6:512 * 6 + 1024], in1=sgb[:], op=ALU.mult),
                   [("ps", 6), ("ps", 7), "sgb"], T1K)
                A2("pool", lambda e: e.tensor_tensor(out=Pt[2][:], in0=t1[:], in1=matt[:], op=ALU.add),
                   T1K + ["matt"], [("Pt", 2)])
                mstage[0] = 3
                for fc in range(8):
                    A2("pe", lambda e, fc=fc: e.transpose(bankb(6)[:, fc * 128:(fc + 1) * 128], Pt[2][:, fc * 128:(fc + 1) * 128], ident_b[:]),
                        [("Pt", 2), "ident_b"], [("ps", 6)])
                A2("act", lambda e: e.copy(Pt[1][:], bankb(6)), [("ps", 6)], [("Pt", 1)])
                mstage[0] = 4
                for nh in range(2):
                    bk = 6 + nh
                    for fc in range(8):
                        A2("pe", lambda e, nh=nh, fc=fc, bk=bk: e.matmul(bank(bk), Pt[1][:, fc * 128:(fc + 1) * 128], wout[:, fc, nh * 512:(nh + 1) * 512],
                                                                        start=(fc == 0), stop=(fc == 7)),
                           [("Pt", 1), "wout"], [("ps", bk)])
                mstage[0] = 5
                A2("dve", lambda e: e.tensor_tensor(out=x2[:], in0=ps[:, 512 * 6:512 * 6 + 1024], in1=x2[:], op=ALU.add),
                   [("ps", 6), ("ps", 7), "x2"], ["x2"])
                A2("act", lambda e: e.activation(out=Pt[0][:], in_=x2[:], func=AF.Square, accum_out=ssq2[:, 0:1]),
                   ["x2"], [("Pt", 0), "ssq2"])
                A2("dve", lambda e: e.tensor_scalar(out=msq[:], in0=ssq2[:], scalar1=1.0 / D, scalar2=RMS_EPS, op0=ALU.mult, op1=ALU.add),
                   ["ssq2"], ["msq"])
                A2("pool", lambda e: e.tensor_tensor(out=rs2[:], in0=msq[:], in1=mhalf[:], op=ALU.pow), ["msq", "mhalf"], ["rs2"])
                A2("pool", lambda e: e.tensor_scalar(out=x2[:], in0=x2[:], scalar1=rs2[:, 0:1], scalar2=0.0, op0=ALU.mult, op1=ALU.add),
                   ["x2", "rs2"], ["x2"])
                A2("pool", lambda e: e.tensor_tensor(out=x2[:], in0=x2[:], in1=fg[:], op=ALU.mult), ["x2", "fg"], ["x2"])
                if sample:
                    A2("sp", lambda e: e.dma_start(out=y_s, in_=x2[0:16, :]), ["x2"], [], "st_y")
                else:
                    A2("sp", lambda e: e.dma_start(out=y_p[bi * 128:(bi + 1) * 128, :], in_=x2[:]), ["x2"], [], "st_y")
                sink[0] = None
                pend_mrg.extend(mrg_new)
                yield "done"

            NI_PER_IT = 20
            if True:
                g = p2_block(NB, True, 0)
                for _ in g:
                    pass
                flushm(None)
                A2("sp", lambda e: e.dma_start(out=KT[:, :, 0:SEQ], in_=KT_d), [("KT_d", i) for i in range(NB)],
                   [("KT", i) for i in range(LK // 128)], "ld_kt")
                A2("sp", lambda e: e.dma_start(out=KIT[:, 0:SEQ], in_=KIT_d), [("KIT_d", i) for i in range(NB)],
                   [("KIT", i) for i in range(LK // 128)], "ld_kit")
                for b0 in range(0, NB, 4):
                    b1 = min(NB, b0 + 4)
                    A2("sp", lambda e, b0=b0, b1=b1: e.dma_start(out=VA[:, b0:b1, :], in_=VA_d[b0:b1].rearrange("i p d -> p i d")),
                       [("VA_d", i) for i in range(b0, b1)], [("VA", i) for i in range(b0, b1)], "ld_va")
                ordr = []
                lo_, hi_ = 0, NB - 1
                while lo_ <= hi_:
                    ordr.append(lo_); lo_ += 1
                    if lo_ <= hi_:
                        ordr.append(hi_); hi_ -= 1
                gens = [p2_block(b, False, (j + 1) % 2) for j, b in enumerate(ordr)]
                next(gens[0])
                for k in range(len(gens)):
                    if k + 1 < len(gens):
                        sink[0] = pend_I
                        next(gens[k + 1])
                        sink[0] = None
                    for _ in gens[k]:
                        pass
                    flushl(pend_I, None)
                flushm(None)

            for e in ("pe", "act", "dve", "pool"):
                c = base_vals[e]
                for op in P2.ops[e]:
                    if op.sig:
                        c += 1
                        op.val = c
            with nc.Block() as blk2:
                @blk2.tensor
                def _(h):
                    P2.emit_engine("pe", h, sems, 0)

                @blk2.scalar
                def _(h):
                    P2.emit_engine("act", h, sems, 0)

                @blk2.vector
                def _(h):
                    P2.emit_engine("dve", h, sems, 0)

                @blk2.gpsimd
                def _(h):
                    P2.emit_engine("pool", h, sems, 0)

                @blk2.sync
                def _(h):
                    P2.emit_engine("sp", h, sems, 0, final_wait=True)
    return nc


_CACHE = {}


def _prep_shared(inputs):
    w_in = np.asarray(inputs["w_in"], np.float32)[0]
    cols = []
    cols += list(range(0, 1536))
    cols += list(range(2048, 2560))
    cols += list(range(3584, 3656))
    for h in Q_HEAD_ORDER:
        cols += list(range(1536 + 64 * h, 1536 + 64 * h + 64))
    cols += list(range(3072, 3584))
    cols += list(range(3584, 3648)) + list(range(3584, 3648))
    cols += list(range(2560, 3072))
    cols += list(range(3656, 4680))
    cols += list(range(4680, 5704))
    assert len(cols) == C_END
    wl = np.ascontiguousarray(w_in[:, cols].reshape(8, 128, C_END).transpose(1, 0, 2))
    w_pa = np.asarray(inputs["w_pa"], np.float32)[0]
    w_pb = np.asarray(inputs["w_pb"], np.float32)[0]
    w_out = np.asarray(inputs["w_out"], np.float32)[0]
    sgu_w = np.asarray(inputs["sgu_w"], np.float32)[0]
    sgu_b = np.asarray(inputs["sgu_b"], np.float32)[0]
    i = np.arange(128)
    sh = {
        "w_in_l": wl,
        "w_pa_l": np.ascontiguousarray(w_pa.reshape(4, 128, D).transpose(1, 0, 2)),
        "w_pb_l": np.ascontiguousarray(w_pb.reshape(4, 128, D).transpose(1, 0, 2)),
        "w_out_l": np.ascontiguousarray(w_out.reshape(8, 128, D).transpose(1, 0, 2)),
        "gcol": np.ascontiguousarray(np.asarray(inputs["norm_g"], np.float32)[0].reshape(8, 128).T),
        "fg_bc": np.ascontiguousarray(np.broadcast_to(np.asarray(inputs["final_g"], np.float32)[None, :], (128, D))),
        "lng_bc": np.ascontiguousarray(np.broadcast_to(np.asarray(inputs["sgu_ln_g"], np.float32)[0][None, :], (128, 512))),
        "lnb_bc": np.ascontiguousarray(np.broadcast_to(np.asarray(inputs["sgu_ln_b"], np.float32)[0][None, :], (128, 512))),
        "bexp": np.ascontiguousarray(np.repeat(sgu_b.T[:, :, None], 64, axis=2).reshape(128, 512)),
        "wsT": np.ascontiguousarray(sgu_w.transpose(2, 0, 1)),
        "mask_p": ((i[:, None] // 64) <= (i[None, :] // 64)).astype(np.float32),
        "mask_s": ((i[:, None] < 16) & (i[None, :] < 16)).astype(np.float32),
        "ident": np.eye(128, dtype=np.float32),
    }
    return sh


def run(inputs, NB=64, n_cores=8):
    if NB not in _CACHE:
        _CACHE[NB] = build(NB)
    nc = _CACHE[NB]
    sh = _prep_shared(inputs)
    xp = np.asarray(inputs["x_prompt"], np.float32)
    xs = np.asarray(inputs["x_sample"], np.float32)
    ck = np.asarray(inputs["cache_k"], np.float32)[0]
    cv = np.asarray(inputs["cache_v"], np.float32)[0]
    cki = np.asarray(inputs["cache_kidx"], np.float32)[0]
    in_maps = []
    for c in range(n_cores):
        m = dict(sh)
        m["xp"] = np.ascontiguousarray(xp[c, :NB * 128])
        xs_pad = np.zeros((128, D), np.float32)
        xs_pad[:16] = xs[c]
        m["xs"] = xs_pad
        m["ck"] = np.ascontiguousarray(ck[c].reshape(1024, 256))
        m["cv"] = np.ascontiguousarray(cv[c].reshape(1024, 256))
        m["cki"] = np.ascontiguousarray(cki[c])
        in_maps.append(m)
    res = run_bass_kernel_spmd(nc, in_maps, core_ids=list(range(n_cores)))
    return res.results


def kernel(**inputs):
    r = run(inputs, 64, 8)
    st = lambda k: np.stack([np.asarray(r[c][k], np.float32) for c in range(8)])
    y_prompt = st("y_p")
    y_sample = st("y_s")
    pk = st("o_pk").reshape(1, 8, 8192, 4, 64)
    pv = st("o_pv").reshape(1, 8, 8192, 4, 64)
    pki = st("o_pki").reshape(1, 8, 8192, 64)
    sk = st("o_sk").reshape(1, 8, 16, 4, 64)
    sv = st("o_sv").reshape(1, 8, 16, 4, 64)
    ski = st("o_ski").reshape(1, 8, 16, 64)
    svn = st("o_svn").reshape(1, 8, 16, 512)
    return (y_prompt, y_sample, pk, pv, pki, sk, sv, ski, svn)
```

```python
import numpy as np
from contextlib import ExitStack
import concourse.bass as bass
import concourse.mybir as mybir
from concourse.bass_utils import run_bass_kernel_spmd

F32 = mybir.dt.float32
BF16 = mybir.dt.bfloat16
AF = mybir.ActivationFunctionType
ALU = mybir.AluOpType

D = 1024
RMS_EPS = 1e-6
LN_EPS = 1e-5
NEG = -1.0e30
TOPK = 256
C_UVZ, C_KV, C_KIWI, C_Q, C_QI, C_KK, C_ZB, C_GA, C_GB, C_END = (
    0, 1536, 2048, 2120, 2632, 3144, 3272, 3784, 4808, 5832)
Q_HEAD_ORDER = [0, 2, 1, 3, 4, 6, 5, 7]
BIS_W = 16.0
BIS_ITERS = 21


class _Op:
    __slots__ = ("eng", "emit", "deps", "sig", "val", "slot", "phase")


class Prog:
    ENG = ("pe", "act", "dve", "pool", "sp")

    def __init__(self):
        self.ops = {e: [] for e in self.ENG}
        self.lastw = {}
        self.readers = {}
        self.slots = {}
        self.slot_last = {}
        self.phase = 0
        self.waited = {e: {} for e in self.ENG}
        self.barrier = None

    def add(self, eng, emit, reads=(), writes=(), slot=None):
        op = _Op()
        op.eng, op.emit, op.sig, op.val, op.slot, op.phase = eng, emit, False, None, slot, self.phase
        is_dma = slot is not None
        deps = {}
        for k in reads:
            w = self.lastw.get(k)
            if w is not None:
                deps[id(w)] = (w, "raw")
        for k in writes:
            w = self.lastw.get(k)
            if w is not None:
                deps[id(w)] = (w, "waw")
            for r in self.readers.get(k, {}).values():
                if id(r) not in deps:
                    deps[id(r)] = (r, "war")
        final = []
        for d, kind in deps.values():
            d_dma = d.slot is not None
            if (not is_dma) and (not d_dma) and d.eng == eng:
                if eng == "pe":
                    continue
            final.append(d)
        if is_dma and slot in self.slot_last:
            pl = self.slot_last[slot]
            if all(pl is not d for d in final):
                final.append(pl)
        if self.barrier is not None and eng in self.barrier:
            final.extend(self.barrier.pop(eng))
        for d in final:
            d.sig = True
        op.deps = final
        for k in reads:
            self.readers.setdefault(k, {})[("dma", id(op)) if is_dma else eng] = op
        for k in writes:
            self.lastw[k] = op
            self.readers[k] = {}
        if is_dma:
            c = self.slots.get(slot, 0) + 16
            self.slots[slot] = c
            op.val = c
            op.sig = True
            self.slot_last[slot] = op
        self.ops[eng].append(op)
        return op

    def set_barrier(self):
        lasts = []
        for e in ("pe", "act", "dve", "pool"):
            if self.ops[e]:
                lasts.append(self.ops[e][-1])
        lasts.extend(self.slot_last.values())
        self.barrier = {e: list(lasts) for e in self.ENG}
        for e in self.ENG:
            self.barrier[e] = [d for d in lasts if not (d.slot is None and d.eng == e)]

    def finalize(self):
        for e in ("pe", "act", "dve", "pool"):
            c = 0
            for op in self.ops[e]:
                if op.sig:
                    c += 1
                    op.val = c

    def emit_engine(self, e, h, sems, phase, final_wait=False):
        waited = self.waited[e]
        for op in self.ops[e]:
            if op.phase != phase:
                continue
            for d in op.deps:
                key = d.slot if d.slot is not None else d.eng
                if waited.get(key, 0) >= d.val:
                    continue
                h.wait_ge(sems[key], d.val)
                waited[key] = d.val
            ins = op.emit(h)
            if op.slot is not None:
                ins.then_inc(sems[op.slot], 16)
            elif op.sig:
                ins.then_inc(sems[e], 1)
        if final_wait and e == "sp":
            for s, v in self.slots.items():
                if waited.get(s, 0) < v:
                    h.wait_ge(sems[s], v)
                    waited[s] = v


def build(NB=64):
    NBT = NB + 1
    SEQ = NB * 128
    LK = max(SEQ, 1152)
    nc = bass.Bass("TRN2", target_bir_lowering=False)

    def din(name, shape, dt=F32):
        return nc.dram_tensor(name, list(shape), dt, kind="ExternalInput").ap()

    def dout(name, shape):
        return nc.dram_tensor(name, list(shape), F32, kind="ExternalOutput").ap()

    def dscr(name, shape, dt):
        return nc.dram_tensor(name, list(shape), dt).ap()

    xp = din("xp", [SEQ, D]); xs = din("xs", [128, D])
    ck = din("ck", [1024, 256]); cv = din("cv", [1024, 256]); cki = din("cki", [1024, 64])
    w_in_l = din("w_in_l", [128, 8, C_END]); w_pa_l = din("w_pa_l", [128, 4, D])
    w_pb_l = din("w_pb_l", [128, 4, D]); w_out_l = din("w_out_l", [128, 8, D])
    gcol_d = din("gcol", [128, 8]); fg_d = din("fg_bc", [128, D])
    lng_d = din("lng_bc", [128, 512]); lnb_d = din("lnb_bc", [128, 512]); bexp_d = din("bexp", [128, 512])
    wsT_d = din("wsT", [128, 8, 128]); mkp_d = din("mask_p", [128, 128]); mks_d = din("mask_s", [128, 128])
    ident_d = din("ident", [128, 128])

    y_p = dout("y_p", [SEQ, D]); y_s = dout("y_s", [16, D])
    o_pk = dout("o_pk", [SEQ, 256]); o_pv = dout("o_pv", [SEQ, 256]); o_pki = dout("o_pki", [SEQ, 64])
    o_sk = dout("o_sk", [16, 256]); o_sv = dout("o_sv", [16, 256]); o_ski = dout("o_ski", [16, 64])
    o_svn = dout("o_svn", [16, 512])

    qT_d = dscr("qT_d", [NBT, 128, 512], BF16); qiT_d = dscr("qiT_d", [NBT, 128, 512], BF16)
    wi_d = dscr("wi_d", [NBT, 128, 8], F32); szb_d = dscr("szb_d", [NBT, 128, 512], BF16)
    sgb_d = dscr("sgb_d", [NBT, 128, 1024], BF16); mat_d = dscr("mat_d", [NBT, 128, 1024], BF16)
    KT_d = dscr("KT_d", [128, 2, SEQ], BF16); VA_d = dscr("VA_d", [NB, 128, 260], BF16)
    KIT_d = dscr("KIT_d", [128, SEQ], BF16)
    KTs_d = dscr("KTs_d", [128, 2, 1152], BF16); VAs_d = dscr("VAs_d", [9, 128, 260], BF16)
    KITs_d = dscr("KITs_d", [128, 1152], BF16)

    P = Prog()
    A = P.add

    with ExitStack() as es0:
        class _Sems(dict):
            def __missing__(self, k):
                v = es0.enter_context(nc.semaphore("s_" + k))
                self[k] = v
                return v
        sems = _Sems()
        ps = es0.enter_context(nc.psum_tensor("ps", [128, 4096], F32))
        psb = ps.bitcast(BF16)

        def bank(k, n=512, p0=0, p1=128):
            return ps[p0:p1, 512 * k:512 * k + n]

        def bankb(k, n=1024):
            return psb[:, 1024 * k:1024 * k + n]

        def sb(es, name, shape, dt):
            return es.enter_context(nc.sbuf_tensor("t_" + name, list(shape), dt))

        ident_f = sb(es0, "ident_f", [128, 128], F32)
        ident_b = sb(es0, "ident_b", [128, 128], BF16)
        mhalf = sb(es0, "mhalf", [128, 1], F32)

        with ExitStack() as es1:
            W = sb(es1, "W", [128, 8, C_END], BF16)
            wpa = sb(es1, "wpa", [128, 4, D], BF16)
            stg = sb(es1, "stg", [128, 8, 512], F32)
            gcol = sb(es1, "gcol", [128, 8], F32)
            lng = sb(es1, "lng", [128, 512], F32); lnb = sb(es1, "lnb", [128, 512], F32)
            bexp = sb(es1, "bexp", [128, 512], F32)
            wsT_f = sb(es1, "wsT_f", [128, 8, 128], F32)
            mk_f = sb(es1, "mk_f", [128, 2, 128], F32)
            wsT_p = sb(es1, "wsT_p", [128, 8, 128], BF16); wsT_s = sb(es1, "wsT_s", [128, 8, 128], BF16)
            xt = [sb(es1, "xt%d" % i, [128, D], F32) for i in range(2)]
            junk = sb(es1, "junk", [128, D], BF16)
            xnb = [sb(es1, "xn%d" % i, [128, D], BF16) for i in range(2)]; hTb = [sb(es1, "hT%d" % i, [128, 8, 128], BF16) for i in range(2)]
            ssq = sb(es1, "ssq", [128, 1], F32); ms = sb(es1, "ms", [128, 1], F32); rstd = sb(es1, "rstd", [128, 1], F32)
            uvb = [sb(es1, "uv%d" % i, [128, 1024], F32) for i in range(2)]; tmpA = sb(es1, "tmpA", [128, 1024], F32)
            zab = [sb(es1, "za%d" % i, [128, 512], F32) for i in range(2)]; kvt = sb(es1, "kvt", [128, 512], F32)
            kiwi = sb(es1, "kiwi", [128, 72], F32); wi_st = sb(es1, "wi_st", [128, 8], F32)
            VAst = sb(es1, "VAst", [128, 4, 65], BF16)
            qT_st = sb(es1, "qT_st", [128, 512], BF16); qiT_st = sb(es1, "qiT_st", [128, 512], BF16)
            KTb = sb(es1, "KTb", [128, 2, 128], BF16); KITb = sb(es1, "KITb", [128, 128], BF16)
            tz = sb(es1, "tz", [128, 512], F32); szb_st = sb(es1, "szb_st", [128, 512], BF16)
            sgab = [sb(es1, "sga%d" % i, [128, 1024], F32) for i in range(2)]; sgbf = sb(es1, "sgbf", [128, 1024], F32)
            sgb_st = sb(es1, "sgb_st", [128, 1024], BF16); mat_st = sb(es1, "mat_st", [128, 1024], BF16)
            stats = sb(es1, "stats", [128, 6], F32); mv = sb(es1, "mv", [128, 2], F32)
            ms2 = sb(es1, "ms2", [128, 1], F32); rstd2 = sb(es1, "rstd2", [128, 1], F32)
            vn = sb(es1, "vn", [128, 512], F32); vnb = sb(es1, "vnb", [128, 512], BF16)
            sz = sb(es1, "sz", [128, 512], F32); a1 = sb(es1, "a1", [128, 512], F32)
            abf = sb(es1, "abf", [128, 512], BF16); aT = sb(es1, "aT", [128, 4, 128], BF16)
            ckt = sb(es1, "ckt", [128, 256], F32); cvt = sb(es1, "cvt", [128, 256], F32)
            ckit = sb(es1, "ckit", [128, 64], F32)
            k16 = sb(es1, "k16", [128, 256], BF16); kk16 = sb(es1, "kk16", [128, 128], BF16)

            _bk = [0]

            def nbank():
                _bk[0] = (_bk[0] + 1) % 8
                return _bk[0]

            def nbank2():
                _bk[0] = ((_bk[0] // 2 + 1) % 4) * 2 + 1
                return _bk[0] - 1

            A("sp", lambda e: e.dma_start(out=ident_f[:], in_=ident_d), [], ["ident_f"], "ldc_id")
            A("dve", lambda e: e.tensor_copy(ident_b[:], ident_f[:]), ["ident_f"], ["ident_b"])
            A("pool", lambda e: e.memset(mhalf[:], -0.5), [], ["mhalf"])
            A("pool", lambda e: e.memset(VAst[:, :, 64:65], 1.0), [], ["VAst1"])
            for nm, t, d_ in (("gcol", gcol, gcol_d), ("lng", lng, lng_d), ("lnb", lnb, lnb_d),
                              ("bexp", bexp, bexp_d), ("wsT_f", wsT_f, wsT_d)):
                A("sp", lambda e, t=t, d_=d_: e.dma_start(out=t[:], in_=d_), [], [nm], "ldc_" + nm)
            A("sp", lambda e: e.dma_start(out=mk_f[:, 0, :], in_=mkp_d), [], ["mk_f0"], "ldc_m0")
            A("sp", lambda e: e.dma_start(out=mk_f[:, 1, :], in_=mks_d), [], ["mk_f1"], "ldc_m1")
            A("dve", lambda e: e.tensor_tensor(out=wsT_p[:], in0=wsT_f[:],
                                               in1=mk_f[:, 0:1, :].to_broadcast([128, 8, 128]), op=ALU.mult),
              ["wsT_f", "mk_f0"], ["wsT_p"])
            A("dve", lambda e: e.tensor_tensor(out=wsT_s[:], in0=wsT_f[:],
                                               in1=mk_f[:, 1:2, :].to_broadcast([128, 8, 128]), op=ALU.mult),
              ["wsT_f", "mk_f1"], ["wsT_s"])
            c0 = 0
            ci = 0
            while c0 < C_END:
                n = min(512, C_END - c0)
                A("sp", lambda e, c0=c0, n=n: e.dma_start(out=stg[:, :, 0:n], in_=w_in_l[:, :, c0:c0 + n]),
                  [], ["stg"], "ldw")
                for c in range(8):
                    if c % 2 == 0:
                        A("dve", lambda e, c=c, c0=c0, n=n: e.tensor_scalar(
                            out=W[:, c, c0:c0 + n], in0=stg[:, c, 0:n], scalar1=gcol[:, c:c + 1], scalar2=None,
                            op0=ALU.mult), ["stg", "gcol"], [("W", ci)])
                    else:
                        A("act", lambda e, c=c, c0=c0, n=n: e.activation(
                            out=W[:, c, c0:c0 + n], in_=stg[:, c, 0:n], func=AF.Copy, scale=gcol[:, c:c + 1]),
                          ["stg", "gcol"], [("W", ci)])
                c0 += n
                ci += 1
            NWC = ci
            Wkeys = [("W", i) for i in range(NWC)]
            A("sp", lambda e: e.dma_start(out=stg[:, 0:4, :], in_=w_pa_l[:, :, 0:512]),
              [], ["stg"], "ldw")
            A("dve", lambda e: e.tensor_copy(wpa[:, :, 0:512], stg[:, 0:4, :]), ["stg"], ["wpa0"])
            A("sp", lambda e: e.dma_start(out=stg[:, 0:4, :], in_=w_pa_l[:, :, 512:1024]), [], ["stg"], "ldw")
            A("dve", lambda e: e.tensor_copy(wpa[:, :, 512:1024], stg[:, 0:4, :]), ["stg"], ["wpa1"])

            cur = {"s": 0}

            def mm_tok(bk, n, c0):
                hT = hTb[cur["s"]]
                for c in range(8):
                    A("pe", lambda e, c=c: e.matmul(bank(bk, n), hT[:, c, :], W[:, c, c0:c0 + n],
                                                    start=(c == 0), stop=(c == 7)),
                      [("hT", cur["s"])] + Wkeys, [("ps", bk)])

            def mm_feat(bk0, nchunks, c0, m=128):
                hT = hTb[cur["s"]]
                for j in range(nchunks):
                    bk = bk0 + (j * 128) // 512
                    off = (j * 128) % 512
                    for c in range(8):
                        A("pe", lambda e, c=c, j=j, bk=bk, off=off: e.matmul(
                            ps[0:m, 512 * bk + off:512 * bk + off + 128], W[:, c, c0 + j * m:c0 + (j + 1) * m],
                            hT[:, c, :], start=(c == 0), stop=(c == 7)),
                          [("hT", cur["s"])] + Wkeys, [("ps", bk)])

            deferred1 = []
            dstage = [0]

            def Ad(*a, **k):
                deferred1.append((dstage[0], a, k))

            def flush1(stage):
                while deferred1 and (stage is None or deferred1[0][0] <= stage):
                    _, a, k = deferred1.pop(0)
                    P.add(*a, **k)

            def p1_front_a(s, x_t):
                xk = ("x", s)
                xn = xnb[s]
                A("act", lambda e: e.activation(out=junk[:], in_=x_t[:], func=AF.Square, accum_out=ssq[:, 0:1]),
                  [xk], ["junk", "ssq"])
                A("dve", lambda e: e.tensor_scalar(out=ms[:], in0=ssq[:], scalar1=1.0 / D, scalar2=RMS_EPS,
                                                   op0=ALU.mult, op1=ALU.add), ["ssq"], ["ms"])
                A("pool", lambda e: e.tensor_tensor(out=rstd[:], in0=ms[:], in1=mhalf[:], op=ALU.pow),
                  ["ms", "mhalf"], ["rstd"])
                A("act", lambda e: e.activation(out=xn[:], in_=x_t[:], func=AF.Copy, scale=rstd[:, 0:1]),
                  [xk, "rstd"], [("xn", s)])

            def p1_front_b(s):
                xn = xnb[s]
                hT_ = hTb[s]
                bk = nbank()
                for c in range(8):
                    A("pe", lambda e, c=c, bk=bk: e.transpose(bankb(bk)[:, c * 128:(c + 1) * 128],
                                                              xn[:, c * 128:(c + 1) * 128], ident_b[:]),
                      [("xn", s), "ident_b"], [("ps", bk)])
                A("dve", lambda e, bk=bk: e.tensor_copy(hT_[:].rearrange("p a b -> p (a b)"), bankb(bk)),
                  [("ps", bk)], [("hT", s)])

            def p1_block(bi, x_ap, s, sample, nxt):
                xk = ("x", s)
                x_t = xt[s]
                uv = uvb[s]; za = zab[s]; sga = sgab[s]
                K_uv0, K_uv1, K_za, K_sga = ("uv0", s), ("uv1", s), ("za", s), ("sga", s)
                cur["s"] = s
                if nxt is not None:
                    p1_front_a(1 - s, xt[1 - s])
                b0 = nbank(); mm_tok(b0, 512, C_UVZ)
                A("act", lambda e, b0=b0: e.copy(uv[:, 0:512], bank(b0)), [("ps", b0)], [K_uv0])
                flush1(0)
                b1 = nbank(); mm_tok(b1, 512, C_UVZ + 512)
                A("dve", lambda e, b1=b1: e.tensor_copy(uv[:, 512:1024], bank(b1)), [("ps", b1)], [K_uv1])
                b2 = nbank(); mm_tok(b2, 512, C_UVZ + 1024)
                A("act", lambda e, b2=b2: e.copy(za[:], bank(b2)), [("ps", b2)], [K_za])
                flush1(0)
                b3 = nbank(); mm_tok(b3, 512, C_KV)
                A("dve", lambda e, b3=b3: e.tensor_copy(kvt[:], bank(b3)), [("ps", b3)], ["kvt"])
                b4 = nbank(); mm_tok(b4, 72, C_KIWI)
                A("act", lambda e, b4=b4: e.copy(kiwi[:], bank(b4, 72)), [("ps", b4)], ["kiwi"])
                if not sample:
                    r0 = bi * 128
                    A("sp", lambda e: e.dma_start(out=o_pk[r0:r0 + 128, :], in_=kvt[:, 0:256]), ["kvt"], [], "st_ok")
                    A("sp", lambda e: e.dma_start(out=o_pv[r0:r0 + 128, :], in_=kvt[:, 256:512]), ["kvt"], [], "st_ov")
                    A("sp", lambda e: e.dma_start(out=o_pki[r0:r0 + 128, :], in_=kiwi[:, 0:64]), ["kiwi"], [], "st_oki")
                else:
                    A("sp", lambda e: e.dma_start(out=o_sk, in_=kvt[0:16, 0:256]), ["kvt"], [], "st_ok")
                    A("sp", lambda e: e.dma_start(out=o_sv, in_=kvt[0:16, 256:512]), ["kvt"], [], "st_ov")
                    A("sp", lambda e: e.dma_start(out=o_ski, in_=kiwi[0:16, 0:64]), ["kiwi"], [], "st_oki")
                A("dve", lambda e: e.tensor_scalar(out=wi_st[:], in0=kiwi[:, 64:72], scalar1=8.0 ** -0.5, scalar2=None,
                                                   op0=ALU.mult), ["kiwi"], ["wi_st"])
                A("sp", lambda e: e.dma_start(out=wi_d[bi], in_=wi_st[:]), ["wi_st"], [("wi_d", bi)], "st_wi")
                A("pool", lambda e: e.tensor_copy(VAst[:, :, 0:64], kvt[:, 256:512].rearrange("p (g d) -> p g d", g=4)),
                  ["kvt"], ["VAst"])
                if not sample:
                    A("sp", lambda e: e.dma_start(out=VA_d[bi], in_=VAst[:].rearrange("p g d -> p (g d)")),
                      ["VAst", "VAst1"], [("VA_d", bi)], "st_va")
                else:
                    A("sp", lambda e: e.dma_start(out=VAs_d[8], in_=VAst[:].rearrange("p g d -> p (g d)")),
                      ["VAst", "VAst1"], [("VAs_d", 8)], "st_va")
                flush1(0)
                bq = nbank(); mm_feat(bq, 4, C_Q)
                A("act", lambda e, bq=bq: e.mul(qT_st[:], bank(bq), 0.125), [("ps", bq)], ["qT_st"])
                A("sp", lambda e: e.dma_start(out=qT_d[bi], in_=qT_st[:]), ["qT_st"], [("qT_d", bi)], "st_q")
                flush1(0)
                bkk = nbank(); mm_feat(bkk, 2, C_KV)
                A("dve", lambda e, bkk=bkk: e.tensor_copy(KTb[:].rearrange("p a b -> p (a b)"), bank(bkk, 256)),
                  [("ps", bkk)], ["KTb"])
                if not sample:
                    A("sp", lambda e: e.dma_start(out=KT_d[:, :, bi * 128:(bi + 1) * 128], in_=KTb[:]),
                      ["KTb"], [("KT_d", bi)], "st_kt")
                else:
                    A("sp", lambda e: e.dma_start(out=KTs_d[:, :, 1024:1152], in_=KTb[:]), ["KTb"], [("KTs_d", 8)], "st_kt")
                flush1(1)
                bqi = nbank(); mm_feat(bqi, 4, C_QI)
                A("act", lambda e, bqi=bqi: e.mul(qiT_st[:], bank(bqi), 0.125), [("ps", bqi)], ["qiT_st"])
                A("sp", lambda e: e.dma_start(out=qiT_d[bi], in_=qiT_st[:]), ["qiT_st"], [("qiT_d", bi)], "st_qi")
                flush1(1)
                bki = nbank(); mm_feat(bki, 1, C_KK)
                A("dve", lambda e, bki=bki: e.tensor_copy(KITb[:], bank(bki, 128)), [("ps", bki)], ["KITb"])
                if not sample:
                    A("sp", lambda e: e.dma_start(out=KIT_d[:, bi * 128:(bi + 1) * 128], in_=KITb[:]),
                      ["KITb"], [("KIT_d", bi)], "st_kit")
                else:
                    A("sp", lambda e: e.dma_start(out=KITs_d[:, 1024:1152], in_=KITb[:]), ["KITb"], [("KITs_d", 8)], "st_kit")
                flush1(2)
                bz = nbank(); mm_tok(bz, 512, C_ZB)
                A("act", lambda e, bz=bz: e.activation(out=tz[:], in_=bank(bz), func=AF.Tanh, scale=0.5),
                  [("ps", bz)], ["tz"])
                A("dve", lambda e: e.tensor_scalar(out=tz[:], in0=tz[:], scalar1=0.5, scalar2=0.5, op0=ALU.mult, op1=ALU.add),
                  ["tz"], ["tz"])
                A("dve", lambda e, bz=bz: e.tensor_tensor(out=szb_st[:], in0=bank(bz), in1=tz[:], op=ALU.mult),
                  [("ps", bz), "tz"], ["szb_st"])
                A("sp", lambda e: e.dma_start(out=szb_d[bi], in_=szb_st[:]), ["szb_st"], [("szb_d", bi)], "st_szb")
                flush1(2)
                bga = nbank2(); mm_tok(bga, 512, C_GA); mm_tok(bga + 1, 512, C_GA + 512)
                gps = ps[:, 512 * bga:512 * bga + 1024]
                A("act", lambda e: e.activation(out=sga[:], in_=gps, func=AF.Tanh, scale=0.5),
                  [("ps", bga), ("ps", bga + 1)], [K_sga])
                A("pool", lambda e: e.tensor_scalar(out=sga[:], in0=sga[:], scalar1=0.5, scalar2=0.5, op0=ALU.mult, op1=ALU.add),
                  [K_sga], [K_sga])
                flush1(3)
                bgb = nbank2(); mm_tok(bgb, 512, C_GB); mm_tok(bgb + 1, 512, C_GB + 512)
                gps2 = ps[:, 512 * bgb:512 * bgb + 1024]
                A("act", lambda e: e.activation(out=sgbf[:], in_=gps2, func=AF.Tanh, scale=0.5),
                  [("ps", bgb), ("ps", bgb + 1)], ["sgbf"])
                A("pool", lambda e: e.tensor_scalar(out=sgb_st[:], in0=sgbf[:], scalar1=0.5, scalar2=0.5, op0=ALU.mult, op1=ALU.add),
                  ["sgbf"], ["sgb_st"])
                A("sp", lambda e: e.dma_start(out=sgb_d[bi], in_=sgb_st[:]), ["sgb_st"], [("sgb_d", bi)], "st_sgb")
                flush1(None)
                dstage[0] = 0
                Ad("act", lambda e: e.activation(out=tmpA[:], in_=uv[:], func=AF.Square), [K_uv0, K_uv1], ["tmpA"])
                Ad("dve", lambda e: e.tensor_scalar(out=tmpA[:], in0=tmpA[:], scalar1=0.044715, scalar2=1.0,
                                                   op0=ALU.mult, op1=ALU.add), ["tmpA"], ["tmpA"])
                Ad("dve", lambda e: e.tensor_tensor(out=tmpA[:], in0=tmpA[:], in1=uv[:], op=ALU.mult),
                  ["tmpA", K_uv0, K_uv1], ["tmpA"])
                Ad("act", lambda e: e.activation(out=tmpA[:], in_=tmpA[:], func=AF.Tanh, scale=0.7978845608028654),
                  ["tmpA"], ["tmpA"])
                Ad("dve", lambda e: e.scalar_tensor_tensor(out=uv[:], in0=tmpA[:], scalar=1.0, in1=uv[:],
                                                          op0=ALU.add, op1=ALU.mult), ["tmpA", K_uv0, K_uv1], [K_uv0, K_uv1])
                Ad("dve", lambda e: e.bn_stats(out=stats[:], in_=uv[:, 512:1024]), [K_uv1], ["stats"])
                Ad("dve", lambda e: e.bn_aggr(out=mv[:], in_=stats[:]), ["stats"], ["mv"])
                Ad("dve", lambda e: e.tensor_scalar(out=ms2[:], in0=mv[:, 1:2], scalar1=4.0 * LN_EPS, scalar2=None, op0=ALU.add),
                  ["mv"], ["ms2"])
                Ad("pool", lambda e: e.tensor_tensor(out=rstd2[:], in0=ms2[:], in1=mhalf[:], op=ALU.pow),
                  ["ms2", "mhalf"], ["rstd2"])
                Ad("dve", lambda e: e.tensor_scalar(out=vn[:], in0=uv[:, 512:1024], scalar1=mv[:, 0:1], scalar2=rstd2[:, 0:1],
                                                   op0=ALU.subtract, op1=ALU.mult), [K_uv1, "mv", "rstd2"], ["vn"])
                Ad("pool", lambda e: e.tensor_tensor(out=vn[:], in0=vn[:], in1=lng[:], op=ALU.mult), ["vn", "lng"], ["vn"])
                Ad("dve", lambda e: e.tensor_tensor(out=vn[:], in0=vn[:], in1=lnb[:], op=ALU.add), ["vn", "lnb"], ["vn"])
                Ad("act", lambda e: e.copy(vnb[:], vn[:]), ["vn"], ["vnb"])
                if sample:
                    Ad("sp", lambda e: e.dma_start(out=o_svn, in_=vn[0:16, :]), ["vn"], [], "st_svn")
                dstage[0] = 1
                wsm = wsT_s if sample else wsT_p
                wk = "wsT_s" if sample else "wsT_p"
                bs = nbank()
                for g in range(8):
                    Ad("pe", lambda e, g=g, bs=bs: e.matmul(bank(bs)[:, g * 64:(g + 1) * 64], wsm[:, g, :],
                                                           vnb[:, g * 64:(g + 1) * 64], start=True, stop=True),
                      [wk, "vnb"], [("ps", bs)])
                Ad("act", lambda e: e.activation(out=sz[:], in_=za[:], func=AF.Tanh, scale=0.5), [K_za], ["sz"])
                Ad("pool", lambda e: e.tensor_scalar(out=sz[:], in0=sz[:], scalar1=0.25, scalar2=0.25, op0=ALU.mult, op1=ALU.add),
                  ["sz"], ["sz"])
                Ad("pool", lambda e: e.tensor_tensor(out=sz[:], in0=sz[:], in1=za[:], op=ALU.mult), ["sz", K_za], ["sz"])
                Ad("dve", lambda e, bs=bs: e.tensor_tensor(out=a1[:], in0=bank(bs), in1=bexp[:], op=ALU.add),
                  [("ps", bs), "bexp"], ["a1"])
                Ad("dve", lambda e: e.tensor_tensor(out=a1[:], in0=a1[:], in1=uv[:, 0:512], op=ALU.mult), ["a1", K_uv0], ["a1"])
                Ad("dve", lambda e: e.tensor_tensor(out=abf[:], in0=a1[:], in1=sz[:], op=ALU.mult), ["a1", "sz"], ["abf"])
                dstage[0] = 2
                bt = nbank()
                for c in range(4):
                    Ad("pe", lambda e, c=c, bt=bt: e.transpose(bankb(bt)[:, c * 128:(c + 1) * 128],
                                                              abf[:, c * 128:(c + 1) * 128], ident_b[:]),
                      ["abf", "ident_b"], [("ps", bt)])
                Ad("act", lambda e, bt=bt: e.copy(aT[:].rearrange("p a b -> p (a b)"), bankb(bt, 512)), [("ps", bt)], ["aT"])
                dstage[0] = 3
                bpa = nbank2()
                for nh in range(2):
                    for c in range(4):
                        Ad("pe", lambda e, c=c, nh=nh: e.matmul(
                            bank(bpa + nh), aT[:, c, :], wpa[:, c, nh * 512:(nh + 1) * 512],
                            start=(c == 0), stop=(c == 3)), ["wpa0", "wpa1", "aT"], [("ps", bpa + nh)])
                Ad("dve", lambda e: e.tensor_tensor(out=mat_st[:], in0=ps[:, 512 * bpa:512 * bpa + 1024], in1=sga[:], op=ALU.mult),
                  [("ps", bpa), ("ps", bpa + 1), K_sga], ["mat_st"])
                Ad("sp", lambda e: e.dma_start(out=mat_d[bi], in_=mat_st[:]), ["mat_st"], [("mat_d", bi)], "st_mat")

            for cb in range(8):
                r0 = cb * 128
                A("sp", lambda e, r0=r0: e.dma_start(out=ckt[:], in_=ck[r0:r0 + 128, :]), [], ["ckt"], "ldck")
                A("sp", lambda e, r0=r0: e.dma_start(out=cvt[:], in_=cv[r0:r0 + 128, :]), [], ["cvt"], "ldcv")
                A("sp", lambda e, r0=r0: e.dma_start(out=ckit[:], in_=cki[r0:r0 + 128, :]), [], ["ckit"], "ldcki")
                A("act", lambda e: e.copy(k16[:], ckt[:]), ["ckt"], ["k16"])
                bk = nbank()
                for c in range(2):
                    A("pe", lambda e, c=c, bk=bk: e.transpose(bankb(bk)[:, c * 128:(c + 1) * 128],
                                                              k16[:, c * 128:(c + 1) * 128], ident_b[:]),
                      ["k16", "ident_b"], [("ps", bk)])
                A("dve", lambda e, bk=bk: e.tensor_copy(KTb[:].rearrange("p a b -> p (a b)"), bankb(bk, 256)),
                  [("ps", bk)], ["KTb"])
                A("sp", lambda e, r0=r0: e.dma_start(out=KTs_d[:, :, r0:r0 + 128], in_=KTb[:]), ["KTb"], [("KTs_d", cb)], "st_kt")
                A("pool", lambda e: e.tensor_copy(VAst[:, :, 0:64], cvt[:].rearrange("p (g d) -> p g d", g=4)), ["cvt"], ["VAst"])
                A("sp", lambda e, cb=cb: e.dma_start(out=VAs_d[cb], in_=VAst[:].rearrange("p g d -> p (g d)")),
                  ["VAst", "VAst1"], [("VAs_d", cb)], "st_va")
                A("act", lambda e: e.copy(kk16[:, 0:64], ckit[:]), ["ckit"], ["kk16a"])
                A("act", lambda e: e.copy(kk16[:, 64:128], ckit[:]), ["ckit"], ["kk16b"])
                bk2 = nbank()
                A("pe", lambda e, bk2=bk2: e.transpose(bankb(bk2)[:, 0:128], kk16[:], ident_b[:]),
                  ["kk16a", "kk16b", "ident_b"], [("ps", bk2)])
                A("dve", lambda e, bk2=bk2: e.tensor_copy(KITb[:], bankb(bk2, 128)), [("ps", bk2)], ["KITb"])
                A("sp", lambda e, r0=r0: e.dma_start(out=KITs_d[:, r0:r0 + 128], in_=KITb[:]), ["KITb"], [("KITs_d", cb)], "st_kit")

            blocks = [(NB, xs, True)] + [(i, xp[i * 128:(i + 1) * 128, :], False) for i in range(NB)]
            A("sp", lambda e: e.dma_start(out=xt[0][:], in_=blocks[0][1]), [], [("x", 0)], "ldx0")
            PIPE_FRONT = True
            if PIPE_FRONT:
                p1_front_a(0, xt[0])
            for n_, (bi, x_ap, sample) in enumerate(blocks):
                s = n_ % 2
                nxt = None
                if n_ + 1 < len(blocks):
                    nx = blocks[n_ + 1][1]
                    nxt = nx if PIPE_FRONT else None
                    A("sp", lambda e, nx=nx, s=s: e.dma_start(out=xt[1 - s][:], in_=nx), [], [("x", 1 - s)], "ldx%d" % (1 - s))
                if not PIPE_FRONT:
                    p1_front_a(s, xt[s])
                p1_front_b(s)
                p1_block(bi, x_ap, s, sample, nxt)
            flush1(None)

            for e_ in ("pe", "act", "dve", "pool"):
                if P.ops[e_]:
                    P.ops[e_][-1].sig = True
            P.finalize()
            _p1_vals = {e: (max([op.val for op in P.ops[e] if op.val is not None] or [0])) for e in ("pe", "act", "dve", "pool")}
            with nc.Block() as blk:
                @blk.tensor
                def _(h):
                    P.emit_engine("pe", h, sems, 0)

                @blk.scalar
                def _(h):
                    P.emit_engine("act", h, sems, 0)

                @blk.vector
                def _(h):
                    P.emit_engine("dve", h, sems, 0)

                @blk.gpsimd
                def _(h):
                    P.emit_engine("pool", h, sems, 0)

                @blk.sync
                def _(h):
                    P.emit_engine("sp", h, sems, 0, final_wait=True)

        P2 = Prog()
        P2.slots = dict(P.slots)
        P2.waited = P.waited
        P2.lastw = {k: v for k, v in P.lastw.items() if isinstance(k, tuple) and isinstance(k[0], str) and k[0].endswith("_d")}
        base_vals = _p1_vals
        lasts = []
        for e in ("pe", "act", "dve", "pool"):
            if P.ops[e]:
                o = _Op(); o.eng, o.slot, o.val, o.sig = e, None, base_vals[e], True
                lasts.append(o)
        for s_, v_ in P.slots.items():
            o = _Op(); o.eng, o.slot, o.val, o.sig = "sp", s_, v_, True
            lasts.append(o)
        P2.barrier = {e: [d for d in lasts if not (d.slot is None and d.eng == e)] for e in Prog.ENG}

        sink = [None]
        pend_mrg = []
        pend_I = []

        mstage = [0]

        def A2(*a, **k):
            if sink[0] is None:
                return P2.add(*a, **k)
            if sink[0] is pend_I:
                sink[0].append((a, k))
            else:
                sink[0].append((mstage[0], a, k))

        def flushl(lst, nmax):
            n_ = len(lst) if nmax is None else min(nmax, len(lst))
            for _ in range(n_):
                a, k = lst.pop(0)
                P2.add(*a, **k)

        def flushm(stage):
            while pend_mrg and (stage is None or pend_mrg[0][0] <= stage):
                _, a, k = pend_mrg.pop(0)
                P2.add(*a, **k)

        def hook(it):
            flushm(it // 3)
            flushl(pend_I, NI_PER_IT)

        with ExitStack() as es2:
            KT = sb(es2, "KT", [128, 2, LK], BF16)
            VA = sb(es2, "VA", [128, LK // 128, 260], BF16)
            KIT = sb(es2, "KIT", [128, LK], BF16)
            SW = max(128 * (NB + 2), 1152)
            S = sb(es2, "S", [128, SW], F32)
            M = sb(es2, "M", [128, SW], BF16)
            wpb = sb(es2, "wpb", [128, 4, D], BF16); wout = sb(es2, "wout", [128, 8, D], BF16)
            fg = sb(es2, "fg", [128, D], F32)
            qT = sb(es2, "qT", [128, 512], BF16); qiT = sb(es2, "qiT", [128, 4, 128], BF16)
            wi = sb(es2, "wi", [128, 8], F32)
            szb = sb(es2, "szb", [128, 512], BF16); sgb = sb(es2, "sgb", [128, 1024], BF16)
            matt = sb(es2, "matt", [128, 1024], BF16); x2 = sb(es2, "x2", [128, D], F32)
            diag = sb(es2, "diag", [128, 8, 128], BF16)
            R = [sb(es2, "R%d" % i, [128, 512], BF16) for i in range(6)]
            Pt = [sb(es2, "Pt%d" % i, [128, 1024], BF16) for i in range(3)]
            Pm = [sb(es2, "Pm%d" % i, [128, 1024], BF16) for i in range(3)]
            MTp = [sb(es2, "MTp%d" % i, [128, 8, 128], BF16) for i in range(2)]
            sm = sb(es2, "sm", [128, 32], F32)
            mx8 = sm[:, 0:8]; tt = sm[:, 8:9]; cnt = sm[:, 9:10]; dd = sm[:, 10:11]
            sgn = sm[:, 11:12]; uu = sm[:, 12:13]; thr = sm[:, 13:14]; thr_c = sm[:, 14:15]
            ssq2 = sm[:, 15:16]; msq = sm[:, 16:17]; rs2 = sm[:, 17:18]; mone = sm[:, 18:19]
            ones_r = sb(es2, "ones_r", [128, 64], F32)
            boT = sb(es2, "boT", [128, 4, 128], BF16); btok = sb(es2, "btok", [128, 512], BF16)
            rdn = sb(es2, "rdn", [128, 8], F32)
            t1 = sb(es2, "t1", [128, 1024], F32)
            T1K = ["t1", "t1a", "t1b"]

            A2("pool", lambda e: e.memset(thr_c[:], -1.0e29), [], ["thr_c"])
            A2("pool", lambda e: e.memset(ones_r[:], 1.0), [], ["ones_r"])
            A2("pool", lambda e: e.memset(mone[:], -1.0), [], ["mone"])
            A2("sp", lambda e: e.dma_start(out=fg[:], in_=fg_d), [], ["fg"], "ldc_fg")
            srcs = [(w_pb_l[:, hh, :], wpb[:, hh, :], "wpb") for hh in range(4)] + \
                   [(w_out_l[:, c, :], wout[:, c, :], "wout") for c in range(8)]
            for n_, (src_, dst_, key_) in enumerate(srcs):
                if n_ % 2 == 0:
                    A2("sp", lambda e, src_=src_: e.dma_start(out=t1[:], in_=src_), [], T1K, "ldw")
                    A2("dve", lambda e, dst_=dst_: e.tensor_copy(dst_, t1[:]), T1K, [key_])
                else:
                    A2("sp", lambda e, src_=src_: e.dma_start(out=x2[:], in_=src_), [], ["x2"], "ldw2")
                    A2("act", lambda e, dst_=dst_: e.copy(dst_, x2[:]), ["x2"], [key_])

            B_S = (0, 1); B_ACC = 2; B_QK = (3, 4); B_PV = (6, 7); B_X = 4
            QK_PAIRS = ((3, 4), (0, 1))
            B_ACC2 = (2, 5)
            B_S4 = (0, 1, 3, 4)
            QK3 = ((0, 1), (2, 3), (4, 5))
            cnts = {"s": 0, "r": 0, "qk": 0, "p": 0, "mt": 0}

            STG = 9

            def p2_block(bi, sample, par):
                nkb = 9 if sample else bi + 1
                L = nkb * 128
                so = 0 if par == 0 else SW - L
                if sample:
                    A2("sp", lambda e: e.dma_start(out=KT[:, :, 0:1152], in_=KTs_d), [("KTs_d", i) for i in range(9)],
                       [("KT", i) for i in range(9)], "ld_kt")
                    A2("sp", lambda e: e.dma_start(out=VA[:, 0:9, :], in_=VAs_d.rearrange("i p d -> p i d")),
                       [("VAs_d", i) for i in range(9)], [("VA", i) for i in range(9)], "ld_va")
                    A2("sp", lambda e: e.dma_start(out=KIT[:, 0:1152], in_=KITs_d), [("KITs_d", i) for i in range(9)],
                       [("KIT", i) for i in range(9)], "ld_kit")
                A2("sp", lambda e: e.dma_start(out=qiT[:].rearrange("p a b -> p (a b)"), in_=qiT_d[bi]), [("qiT_d", bi)], ["qiT"], "ld_qi")
                A2("sp", lambda e: e.dma_start(out=wi[:], in_=wi_d[bi]), [("wi_d", bi)], ["wi"], "ld_wi")
                xsrc = xs if sample else xp[bi * 128:(bi + 1) * 128, :]

                def loads_b():
                    A2("sp", lambda e: e.dma_start(out=qT[:], in_=qT_d[bi]), [("qT_d", bi)], ["qT"], "ld_q")
                    A2("sp", lambda e: e.dma_start(out=szb[:], in_=szb_d[bi]), [("szb_d", bi)], ["szb"], "ld_szb")
                    A2("sp", lambda e: e.dma_start(out=sgb[:], in_=sgb_d[bi]), [("sgb_d", bi)], ["sgb"], "ld_sgb")
                    A2("sp", lambda e: e.dma_start(out=matt[:], in_=mat_d[bi]), [("mat_d", bi)], ["matt"], "ld_mat")
                    A2("sp", lambda e: e.dma_start(out=x2[:], in_=xsrc), [], ["x2"], "ld_x2")
                for h in range(8):
                    eng = "pool" if h % 2 else "dve"
                    A2(eng, lambda e, h=h: e.tensor_scalar(out=diag[:, h, :], in0=ident_b[:], scalar1=wi[:, h:h + 1], scalar2=0.0,
                                                          op0=ALU.mult, op1=ALU.add), ["wi", "ident_b"], [("diag", h)])
                items = []
                for kt in range((L + 511) // 512):
                    k0 = kt * 512
                    n = min(512, L - k0)
                    for h in range(8):
                        items.append((kt, k0, n, h))

                def idx_mm1(ii):
                    kt, k0, n, h = items[ii]
                    bs_ = B_S4[(h % 2) + 2 * ((ii // 2) % 2)]
                    p0 = 64 * (h % 2)
                    kbs = range(k0 // 128, (k0 + n) // 128)
                    A2("pe", lambda e: e.matmul(bank(bs_, n), qiT[p0:p0 + 64, h // 2, :], KIT[p0:p0 + 64, k0:k0 + n],
                                                start=True, stop=True),
                       ["qiT"] + [("KIT", kb) for kb in kbs], [("ps", bs_)])
                    r_ = cnts["r"] % 6; cnts["r"] += 1
                    if h % 4 != 1:
                        A2("act", lambda e: e.activation(out=R[r_][:, 0:n], in_=bank(bs_, n), func=AF.Relu),
                           [("ps", bs_)], [("R", r_)])
                    else:
                        A2("dve", lambda e: e.tensor_scalar(out=R[r_][:, 0:n], in0=bank(bs_, n), scalar1=0.0, scalar2=None,
                                                            op0=ALU.max), [("ps", bs_)], [("R", r_)])
                    return r_

                def idx_mm2(ii, r_):
                    kt, k0, n, h = items[ii]
                    ba = B_ACC2[kt % 2]
                    A2("pe", lambda e: e.matmul(bank(ba, n), diag[:, h, :], R[r_][:, 0:n], start=(h == 0), stop=(h == 7)),
                       [("diag", h), ("R", r_)], [("ps", ba)])
                    if h == 7:
                        if kt % 2 == 0:
                            A2("act", lambda e: e.copy(S[:, so + k0:so + k0 + n], bank(ba, n)), [("ps", ba)], [("S", par, kt)])
                        else:
                            A2("dve", lambda e: e.tensor_copy(S[:, so + k0:so + k0 + n], bank(ba, n)), [("ps", ba)], [("S", par, kt)])

                rr = {}
                npair = len(items) // 2
                AHP = 2
                for pj in range(npair + AHP):
                    if pj < npair:
                        rr[2 * pj] = idx_mm1(2 * pj)
                        rr[2 * pj + 1] = idx_mm1(2 * pj + 1)
                    if pj >= AHP:
                        q_ = pj - AHP
                        idx_mm2(2 * q_, rr[2 * q_])
                        idx_mm2(2 * q_ + 1, rr[2 * q_ + 1])
                SK = [("S", par, kt) for kt in range((L + 511) // 512)]
                KMd, KMa, KM = ("Md", par), ("Ma", par), ("M", par)
                if STG < 2:
                    return
                if sample:
                    A2("pool", lambda e: e.memset(S[:, so + 1040:so + 1152], NEG), SK, SK)
                else:
                    A2("pool", lambda e: e.memset(S[0:64, so + L - 64:so + L], NEG), SK, SK)
                yield "I"
                if L <= TOPK:
                    thr_t, thr_k = thr_c, "thr_c"
                else:
                    A2("dve", lambda e: e.max(out=mx8[:], in_=S[:, so:so + L]), SK, ["mx8"])
                    w = BIS_W / 2.0
                    A2("dve", lambda e, w=w: e.tensor_scalar(out=tt[:], in0=mx8[:, 0:1], scalar1=-BIS_W + w, scalar2=None, op0=ALU.add),
                       ["mx8"], ["tt"])
                    if L >= 768:
                        Ld = int(round(L * 0.47 / 128.0)) * 128
                    else:
                        Ld = L
                    nA = L - Ld
                    for it in range(BIS_ITERS):
                        wn = w / 2.0
                        A2("dve", lambda e: e.tensor_scalar(out=M[:, so:so + Ld], in0=S[:, so:so + Ld], scalar1=tt[:, 0:1], scalar2=None,
                                                            op0=ALU.is_ge, op1=ALU.add, accum_out=cnt[:, 0:1]),
                           SK + ["tt"], [KMd, "cnt"])
                        if nA > 0:
                            A2("act", lambda e: e.activation(out=M[:, so + Ld:so + L], in_=S[:, so + Ld:so + L], func=AF.Sign, bias=tt[:, 0:1], scale=-1.0,
                                                             accum_out=sgn[:, 0:1]), SK + ["tt"], [KMa, "sgn"])
                            A2("dve", lambda e: e.scalar_tensor_tensor(out=uu[:], in0=cnt[:], scalar=2.0, in1=sgn[:],
                                                                       op0=ALU.mult, op1=ALU.subtract), ["cnt", "sgn"], ["uu"])
                            A2("dve", lambda e, wn=wn: e.tensor_scalar(out=dd[:], in0=uu[:], scalar1=510.5 - nA, scalar2=2.0 * wn,
                                                                      op0=ALU.is_ge, op1=ALU.mult), ["uu"], ["dd"])
                        else:
                            A2("dve", lambda e, wn=wn: e.tensor_scalar(out=dd[:], in0=cnt[:], scalar1=float(TOPK), scalar2=2.0 * wn,
                                                                      op0=ALU.is_ge, op1=ALU.mult), ["cnt"], ["dd"])
                        A2("dve", lambda e, wn=wn: e.scalar_tensor_tensor(out=tt[:], in0=dd[:], scalar=-wn, in1=tt[:],
                                                                         op0=ALU.add, op1=ALU.add), ["dd", "tt"], ["tt"])
                        w = wn
                        hook(it)
                    A2("dve", lambda e, w=w: e.tensor_scalar(out=thr[:], in0=tt[:], scalar1=-w, scalar2=None, op0=ALU.add),
                       ["tt"], ["thr"])
                    thr_t, thr_k = thr, "thr"
                A2("dve", lambda e: e.tensor_scalar(out=M[:, so:so + L], in0=S[:, so:so + L], scalar1=thr_t[:, 0:1], scalar2=None, op0=ALU.is_ge),
                   SK + [thr_k], [KMd, KMa, KM])
                flushm(None)
                flushl(pend_I, None)
                if STG < 3:
                    return
                loads_b()
                qT4 = qT[:].rearrange("p (a j q) -> p a j q", a=2, j=2)
                first = [True, True]
                npc = (nkb + 7) // 8
                mt_slot = {}

                def att_mt(pc):
                    kb0 = pc * 8
                    nj = min(8, nkb - kb0)
                    ms_ = cnts["mt"] % 2; cnts["mt"] += 1
                    mt_slot[pc] = ms_
                    for j in range(nj):
                        A2("pe", lambda e, j=j: e.transpose(bankb(B_X)[:, j * 128:(j + 1) * 128],
                                                            M[:, so + (kb0 + j) * 128:so + (kb0 + j + 1) * 128], ident_b[:]),
                           [KM, "ident_b"], [("ps", B_X)])
                    A2("act", lambda e: e.copy(MTp[ms_][:].rearrange("p a b -> p (a b)")[:, 0:nj * 128],
                                               bankb(B_X, nj * 128)), [("ps", B_X)], [("MT", ms_)])

                def att_qk(kb):
                    pair = QK3[kb % 3]
                    r_ = kb % 3
                    for p in range(2):
                        for gi in range(2):
                            p0 = 64 * gi
                            A2("pe", lambda e, p=p, gi=gi, p0=p0: e.matmul(
                                bank(pair[gi])[:, p * 256:(p + 1) * 256], KT[p0:p0 + 64, p, kb * 128:(kb + 1) * 128],
                                qT4[p0:p0 + 64, p, :, :], start=True, stop=True),
                               [("KT", kb), "qT"], [("ps", pair[gi])])
                    A2("act", lambda e: e.activation(out=Pt[r_][:], in_=ps[:, 512 * pair[0]:512 * pair[0] + 1024], func=AF.Exp),
                       [("ps", pair[0]), ("ps", pair[1])], [("Pt", r_)])
                    ms_ = mt_slot[kb // 8]
                    j = kb % 8
                    A2("dve", lambda e: e.tensor_tensor(
                        out=Pm[r_][:].rearrange("p (h q) -> p h q", h=8), in0=Pt[r_][:].rearrange("p (h q) -> p h q", h=8),
                        in1=MTp[ms_][:, j:j + 1, :].to_broadcast([128, 8, 128]), op=ALU.mult),
                       [("Pt", r_), ("MT", ms_)], [("Pm", r_)])

                def att_pv(kb):
                    r_ = kb % 3
                    for h in range(8):
                        p, gi, j = h // 4, (h % 4) // 2, h % 2
                        g = h // 2
                        bkv = B_PV[h // 4]
                        st = first[h // 4]
                        first[h // 4] = False
                        col = gi * 512 + p * 256 + j * 128
                        A2("pe", lambda e, h=h, g=g, bkv=bkv, st=st, col=col: e.matmul(
                            ps[:, 512 * bkv + (h % 4) * 65:512 * bkv + (h % 4) * 65 + 65],
                            Pm[r_][:, col:col + 128], VA[:, kb, g * 65:(g + 1) * 65],
                            start=st, stop=(kb == nkb - 1 and h % 4 == 3), skip_group_check=True),
                           [("VA", kb), ("Pm", r_)], [("ps", bkv)])

                att_mt(0)
                att_qk(0)
                if nkb > 1:
                    att_qk(1)
                for kb in range(nkb):
                    if kb % 8 == 0 and kb // 8 + 1 < npc:
                        att_mt(kb // 8 + 1)
                    if kb + 2 < nkb:
                        att_qk(kb + 2)
                    att_pv(kb)
                if STG < 4:
                    return
                mrg_new = []
                sink[0] = mrg_new
                mstage[0] = 0
                for p in range(2):
                    pv3 = ps[:, 512 * B_PV[p]:512 * B_PV[p] + 260].rearrange("q (h d) -> q h d", h=4)
                    A2("dve", lambda e, p=p, pv3=pv3: e.reciprocal(out=rdn[:, 4 * p:4 * p + 4].rearrange("q (h o) -> q h o", o=1),
                                                                    in_=pv3[:, :, 64:65]),
                        [("ps", B_PV[p])], [("rdn", p)])
                    A2("dve", lambda e, p=p, pv3=pv3: e.tensor_tensor(
                        out=t1[:, p * 256:(p + 1) * 256].rearrange("q (h d) -> q h d", h=4), in0=pv3[:, :, 0:64],
                        in1=rdn[:, 4 * p:4 * p + 4].rearrange("q (h o) -> q h o", o=1).to_broadcast([128, 4, 64]), op=ALU.mult),
                        [("ps", B_PV[p]), ("rdn", p)], ["t1a"])
                A2("pool", lambda e: e.tensor_tensor(out=btok[:], in0=t1[:, 0:512], in1=szb[:], op=ALU.mult),
                    ["t1a", "szb"], ["btok"])
                mstage[0] = 1
                for c in range(4):
                    A2("pe", lambda e, c=c: e.transpose(bankb(7)[:, c * 128:(c + 1) * 128], btok[:, c * 128:(c + 1) * 128], ident_b[:]),
                        ["btok", "ident_b"], [("ps", 7)])
                A2("act", lambda e: e.copy(boT[:].rearrange("p a b -> p (a b)"), bankb(7, 512)), [("ps", 7)], ["boT"])
                if STG < 5:
                    return
                mstage[0] = 2
                for nh in range(2):
                    bk = 6 + nh
                    for c in range(4):
                        A2("pe", lambda e, nh=nh, c=c, bk=bk: e.matmul(
                            bank(bk), boT[:, c, :], wpb[:, c, nh * 512:(nh + 1) * 512],
                            start=(c == 0), stop=(c == 3)), ["wpb", "boT"], [("ps", bk)])
                A2("dve", lambda e: e.tensor_tensor(out=t1[:], in0=ps[:, 512 * 6:512 * 6 + 1024], in1=sgb[:], op=ALU.mult),
                   [("ps", 6), ("ps", 7), "sgb"], T1K)
                A2("pool", lambda e: e.tensor_tensor(out=Pt[2][:], in0=t1[:], in1=matt[:], op=ALU.add),
                   T1K + ["matt"], [("Pt", 2)])
                mstage[0] = 3
                for fc in range(8):
                    A2("pe", lambda e, fc=fc: e.transpose(bankb(6)[:, fc * 128:(fc + 1) * 128], Pt[2][:, fc * 128:(fc + 1) * 128], ident_b[:]),
                        [("Pt", 2), "ident_b"], [("ps", 6)])
                A2("act", lambda e: e.copy(Pt[1][:], bankb(6)), [("ps", 6)], [("Pt", 1)])
                mstage[0] = 4
                for nh in range(2):
                    bk = 6 + nh
                    for fc in range(8):
                        A2("pe", lambda e, nh=nh, fc=fc, bk=bk: e.matmul(bank(bk), Pt[1][:, fc * 128:(fc + 1) * 128], wout[:, fc, nh * 512:(nh + 1) * 512],
                                                                        start=(fc == 0), stop=(fc == 7)),
                           [("Pt", 1), "wout"], [("ps", bk)])
                mstage[0] = 5
                A2("dve", lambda e: e.tensor_tensor(out=x2[:], in0=ps[:, 512 * 6:512 * 6 + 1024], in1=x2[:], op=ALU.add),
                   [("ps", 6), ("ps", 7), "x2"], ["x2"])
                A2("act", lambda e: e.activation(out=Pt[0][:], in_=x2[:], func=AF.Square, accum_out=ssq2[:, 0:1]),
                   ["x2"], [("Pt", 0), "ssq2"])
                A2("dve", lambda e: e.tensor_scalar(out=msq[:], in0=ssq2[:], scalar1=1.0 / D, scalar2=RMS_EPS, op0=ALU.mult, op1=ALU.add),
                   ["ssq2"], ["msq"])
                A2("pool", lambda e: e.tensor_tensor(out=rs2[:], in0=msq[:], in1=mhalf[:], op=ALU.pow), ["msq", "mhalf"], ["rs2"])
                A2("pool", lambda e: e.tensor_scalar(out=x2[:], in0=x2[:], scalar1=rs2[:, 0:1], scalar2=0.0, op0=ALU.mult, op1=ALU.add),
                   ["x2", "rs2"], ["x2"])
                A2("pool", lambda e: e.tensor_tensor(out=x2[:], in0=x2[:], in1=fg[:], op=ALU.mult), ["x2", "fg"], ["x2"])
                if sample:
                    A2("sp", lambda e: e.dma_start(out=y_s, in_=x2[0:16, :]), ["x2"], [], "st_y")
                else:
                    A2("sp", lambda e: e.dma_start(out=y_p[bi * 128:(bi + 1) * 128, :], in_=x2[:]), ["x2"], [], "st_y")
                sink[0] = None
                pend_mrg.extend(mrg_new)
                yield "done"

            NI_PER_IT = 20
            if True:
                A2("sp", lambda e: e.dma_start(out=KT[:, :, 0:SEQ], in_=KT_d), [("KT_d", i) for i in range(NB)],
                   [("KT", i) for i in range(LK // 128)], "ld_kt")
                A2("sp", lambda e: e.dma_start(out=KIT[:, 0:SEQ], in_=KIT_d), [("KIT_d", i) for i in range(NB)],
                   [("KIT", i) for i in range(LK // 128)], "ld_kit")
                for b0 in range(0, NB, 4):
                    b1 = min(NB, b0 + 4)
                    A2("sp", lambda e, b0=b0, b1=b1: e.dma_start(out=VA[:, b0:b1, :], in_=VA_d[b0:b1].rearrange("i p d -> p i d")),
                       [("VA_d", i) for i in range(b0, b1)], [("VA", i) for i in range(b0, b1)], "ld_va")
                ordr = []
                lo_, hi_ = 0, NB - 1
                while lo_ <= hi_:
                    ordr.append(lo_); lo_ += 1
                    if lo_ <= hi_:
                        ordr.append(hi_); hi_ -= 1
                gens = [p2_block(b, False, (j + 1) % 2) for j, b in enumerate(ordr)]
                next(gens[0])
                for k in range(len(gens)):
                    if k + 1 < len(gens):
                        sink[0] = pend_I
                        next(gens[k + 1])
                        sink[0] = None
                    for _ in gens[k]:
                        pass
                    flushl(pend_I, None)
                flushm(None)
                g = p2_block(NB, True, 0)
                for _ in g:
                    pass
                flushm(None)

            for e in ("pe", "act", "dve", "pool"):
                c = base_vals[e]
                for op in P2.ops[e]:
                    if op.sig:
                        c += 1
                        op.val = c
            with nc.Block() as blk2:
                @blk2.tensor
                def _(h):
                    P2.emit_engine("pe", h, sems, 0)

                @blk2.scalar
                def _(h):
                    P2.emit_engine("act", h, sems, 0)

                @blk2.vector
                def _(h):
                    P2.emit_engine("dve", h, sems, 0)

                @blk2.gpsimd
                def _(h):
                    P2.emit_engine("pool", h, sems, 0)

                @blk2.sync
                def _(h):
                    P2.emit_engine("sp", h, sems, 0, final_wait=True)
    return nc


_CACHE = {}


def _prep_shared(inputs):
    w_in = np.asarray(inputs["w_in"], np.float32)[0]
    cols = []
    cols += list(range(0, 1536))
    cols += list(range(2048, 2560))
    cols += list(range(3584, 3656))
    for h in Q_HEAD_ORDER:
        cols += list(range(1536 + 64 * h, 1536 + 64 * h + 64))
    cols += list(range(3072, 3584))
    cols += list(range(3584, 3648)) + list(range(3584, 3648))
    cols += list(range(2560, 3072))
    cols += list(range(3656, 4680))
    cols += list(range(4680, 5704))
    assert len(cols) == C_END
    wl = np.ascontiguousarray(w_in[:, cols].reshape(8, 128, C_END).transpose(1, 0, 2))
    w_pa = np.asarray(inputs["w_pa"], np.float32)[0]
    w_pb = np.asarray(inputs["w_pb"], np.float32)[0]
    w_out = np.asarray(inputs["w_out"], np.float32)[0]
    sgu_w = np.asarray(inputs["sgu_w"], np.float32)[0]
    sgu_b = np.asarray(inputs["sgu_b"], np.float32)[0]
    i = np.arange(128)
    sh = {
        "w_in_l": wl,
        "w_pa_l": np.ascontiguousarray(w_pa.reshape(4, 128, D).transpose(1, 0, 2)),
        "w_pb_l": np.ascontiguousarray(w_pb.reshape(4, 128, D).transpose(1, 0, 2)),
        "w_out_l": np.ascontiguousarray(w_out.reshape(8, 128, D).transpose(1, 0, 2)),
        "gcol": np.ascontiguousarray(np.asarray(inputs["norm_g"], np.float32)[0].reshape(8, 128).T),
        "fg_bc": np.ascontiguousarray(np.broadcast_to(np.asarray(inputs["final_g"], np.float32)[None, :], (128, D))),
        "lng_bc": np.ascontiguousarray(np.broadcast_to(np.asarray(inputs["sgu_ln_g"], np.float32)[0][None, :], (128, 512))),
        "lnb_bc": np.ascontiguousarray(np.broadcast_to(np.asarray(inputs["sgu_ln_b"], np.float32)[0][None, :], (128, 512))),
        "bexp": np.ascontiguousarray(np.repeat(sgu_b.T[:, :, None], 64, axis=2).reshape(128, 512)),
        "wsT": np.ascontiguousarray(sgu_w.transpose(2, 0, 1)),
        "mask_p": ((i[:, None] // 64) <= (i[None, :] // 64)).astype(np.float32),
        "mask_s": ((i[:, None] < 16) & (i[None, :] < 16)).astype(np.float32),
        "ident": np.eye(128, dtype=np.float32),
    }
    return sh


def run(inputs, NB=64, n_cores=8):
    if NB not in _CACHE:
        _CACHE[NB] = build(NB)
    nc = _CACHE[NB]
    sh = _prep_shared(inputs)
    xp = np.asarray(inputs["x_prompt"], np.float32)
    xs = np.asarray(inputs["x_sample"], np.float32)
    ck = np.asarray(inputs["cache_k"], np.float32)[0]
    cv = np.asarray(inputs["cache_v"], np.float32)[0]
    cki = np.asarray(inputs["cache_kidx"], np.float32)[0]
    in_maps = []
    for c in range(n_cores):
        m = dict(sh)
        m["xp"] = np.ascontiguousarray(xp[c, :NB * 128])
        xs_pad = np.zeros((128, D), np.float32)
        xs_pad[:16] = xs[c]
        m["xs"] = xs_pad
        m["ck"] = np.ascontiguousarray(ck[c].reshape(1024, 256))
        m["cv"] = np.ascontiguousarray(cv[c].reshape(1024, 256))
        m["cki"] = np.ascontiguousarray(cki[c])
        in_maps.append(m)
    res = run_bass_kernel_spmd(nc, in_maps, core_ids=list(range(n_cores)))
    return res.results


def kernel(**inputs):
    r = run(inputs, 64, 8)
    st = lambda k: np.stack([np.asarray(r[c][k], np.float32) for c in range(8)])
    y_prompt = st("y_p")
    y_sample = st("y_s")
    pk = st("o_pk").reshape(1, 8, 8192, 4, 64)
    pv = st("o_pv").reshape(1, 8, 8192, 4, 64)
    pki = st("o_pki").reshape(1, 8, 8192, 64)
    sk = st("o_sk").reshape(1, 8, 16, 4, 64)
    sv = st("o_sv").reshape(1, 8, 16, 4, 64)
    ski = st("o_ski").reshape(1, 8, 16, 64)
    svn = st("o_svn").reshape(1, 8, 16, 512)
    return (y_prompt, y_sample, pk, pv, pki, sk, sv, ski, svn)
```
